# Optimizing a Trainium2 kernel written in Bass

```python
import math
import jax, jax.numpy as jnp
from jax import lax
import numpy as np

D_MODEL = 1024
BATCH = 4
SEQ = 8192
DEPTH = 1

HEAD_DIM = 128
HEADS_PER_GROUP = 4
DILATED_GROUPS = ((128, 1), (512, 4), (2048, 16))
N_GROUPS = 3
N_ATTN_HEADS = N_GROUPS * HEADS_PER_GROUP
ATTN_QKV_WIDTH = N_ATTN_HEADS * HEAD_DIM
ATTN_OUT_WIDTH = HEADS_PER_GROUP * HEAD_DIM
BAND_BLOCK = 128
LRU_WIDTH = D_MODEL
LRU_HEADS = 4
LRU_HEAD_DIM = LRU_WIDTH // LRU_HEADS
CONV_WIDTH = 4
LRU_C = 8.0
D_FF = 2816
REL_BUCKETS = 32
REL_MAX_DISTANCE = 2048
NORM_EPS = 1e-6
N_BRANCHES = 2
W_IN_COLS = 3 * ATTN_QKV_WIDTH + 2 * LRU_WIDTH + N_BRANCHES * D_MODEL

kernel_name = "hybrid_dilated_attn_rglru_macaron"


def rms_norm(x, g):
    xf = x.astype(jnp.float32)
    y = xf * lax.rsqrt(jnp.mean(xf * xf, axis=-1, keepdims=True) + NORM_EPS)
    return (y * g.astype(jnp.float32)).astype(x.dtype)


def swiglu(x, w_gate, w_up, w_down):
    return (jax.nn.silu(x @ w_gate) * (x @ w_up)) @ w_down


def t5_causal_bucket(distance):
    max_exact = REL_BUCKETS // 2
    nf = jnp.maximum(distance, 1).astype(jnp.float32)
    large = max_exact + (jnp.log(nf / max_exact) / math.log(REL_MAX_DISTANCE / max_exact)
                         * (REL_BUCKETS - max_exact)).astype(jnp.int32)
    large = jnp.minimum(large, REL_BUCKETS - 1)
    return jnp.where(distance < max_exact, distance, large)


def dilated_group_attention(q, k, v, bias_table_g, window, dilation):
    B, S, H, E = q.shape
    d = dilation
    L = S // d
    nb = -(-L // BAND_BLOCK)
    Lp = nb * BAND_BLOCK
    m_max = window // d

    def to_blocks(t):
        t = t.reshape(B, L, d, H, E)
        t = jnp.pad(t, ((0, 0), (0, Lp - L), (0, 0), (0, 0), (0, 0)))
        return t.reshape(B, nb, BAND_BLOCK, d, H, E)

    def band(t):
        tb = jnp.pad(to_blocks(t), ((0, 0), (1, 0), (0, 0), (0, 0), (0, 0), (0, 0)))
        return jnp.concatenate([tb[:, :-1], tb[:, 1:]], axis=2)

    qb = to_blocks(q)
    kb = band(k)
    vb = band(v)

    qi = jnp.arange(BAND_BLOCK)[:, None]
    kj = jnp.arange(2 * BAND_BLOCK)[None, :]
    steps = qi + BAND_BLOCK - kj
    key_idx = jnp.arange(nb)[:, None, None] * BAND_BLOCK - BAND_BLOCK + kj[None]
    valid = (steps >= 0) & (steps <= m_max) & (key_idx >= 0)
    bucket = t5_causal_bucket(jnp.maximum(steps, 0) * d)
    bias = jnp.transpose(bias_table_g[bucket], (2, 0, 1)).astype(jnp.float32)

    scale = 1.0 / math.sqrt(E)
    scores = jnp.einsum('bnqphe,bnkphe->bphnqk', qb, kb).astype(jnp.float32) * scale
    scores = scores + bias[None, None, :, None]
    scores = jnp.where(valid[None, None, None], scores, -jnp.inf)
    mx = jnp.max(scores, axis=-1, keepdims=True)
    e = jnp.exp(scores - mx)
    den = jnp.sum(e, axis=-1, keepdims=True)
    o = jnp.einsum('bphnqk,bnkphe->bnqphe', (e / den).astype(v.dtype), vb)
    lse = (mx + jnp.log(den))[..., 0]
    o = o.reshape(B, Lp, d, H, E)[:, :L].reshape(B, S, H, E)
    lse = jnp.transpose(lse, (0, 3, 4, 1, 2)).reshape(B, Lp, d, H)[:, :L].reshape(B, S, H)
    return o, lse


def rg_lru_branch(xr, conv_w, conv_b, w_x, b_x, w_a, b_a, a_param):
    B, S, C = xr.shape
    xc = lax.conv_general_dilated(
        xr, conv_w[:, None, :].astype(xr.dtype), window_strides=(1,),
        padding=((CONV_WIDTH - 1, 0),), dimension_numbers=('NWC', 'WIO', 'NWC'),
        feature_group_count=C) + conv_b
    xh = xc.reshape(B, S, LRU_HEADS, LRU_HEAD_DIM)
    gate_x = jax.nn.sigmoid(jnp.einsum('bshi,hij->bshj', xh, w_x) + b_x).reshape(B, S, C)
    gate_a = jax.nn.sigmoid(jnp.einsum('bshi,hij->bshj', xh, w_a) + b_a).reshape(B, S, C)
    log_a = -LRU_C * gate_a.astype(jnp.float32) * jax.nn.softplus(-a_param.astype(jnp.float32))
    a = jnp.exp(log_a)
    b = jnp.sqrt(-jnp.expm1(2.0 * log_a)) * (gate_x * xc).astype(jnp.float32)

    def combine(left, right):
        a_l, b_l = left
        a_r, b_r = right
        return a_l * a_r, a_r * b_l + b_r

    _, h = lax.associative_scan(combine, (a, b), axis=1)
    return h.astype(xr.dtype)


def setup_inputs(seed: int = 0) -> dict:
    key = jax.random.key(seed)
    ks = jax.random.split(key, 32)
    f32 = jnp.float32

    def w(k, shape, fan_in):
        return jax.random.normal(k, shape, f32) * (fan_in ** -0.5)

    def gain(k):
        return 1.0 + 0.05 * jax.random.normal(k, (DEPTH, D_MODEL), f32)

    a0 = jax.random.uniform(ks[20], (DEPTH, LRU_WIDTH), f32, 0.9, 0.999)
    s = a0 ** (1.0 / LRU_C)
    a_param = jnp.log(s) - jnp.log1p(-s)

    return {
        "x": jax.random.normal(ks[0], (BATCH, SEQ, D_MODEL), f32),
        "ffn1_norm_pre": gain(ks[1]),
        "ffn1_norm_post": gain(ks[2]),
        "ffn1_w_gate": w(ks[3], (DEPTH, D_MODEL, D_FF), D_MODEL),
        "ffn1_w_up": w(ks[4], (DEPTH, D_MODEL, D_FF), D_MODEL),
        "ffn1_w_down": w(ks[5], (DEPTH, D_FF, D_MODEL), D_FF),
        "mix_norm_pre": gain(ks[6]),
        "mix_norm_post": gain(ks[7]),
        "w_in": w(ks[8], (DEPTH, D_MODEL, W_IN_COLS), D_MODEL),
        "rel_bias_table": 0.2 * jax.random.normal(ks[9], (REL_BUCKETS, N_ATTN_HEADS), f32),
        "conv_w": w(ks[10], (DEPTH, CONV_WIDTH, LRU_WIDTH), CONV_WIDTH),
        "conv_b": 0.02 * jax.random.normal(ks[11], (DEPTH, LRU_WIDTH), f32),
        "lru_w_x": w(ks[12], (DEPTH, LRU_HEADS, LRU_HEAD_DIM, LRU_HEAD_DIM), LRU_HEAD_DIM),
        "lru_b_x": 0.02 * jax.random.normal(ks[13], (DEPTH, LRU_HEADS, LRU_HEAD_DIM), f32),
        "lru_w_a": w(ks[14], (DEPTH, LRU_HEADS, LRU_HEAD_DIM, LRU_HEAD_DIM), LRU_HEAD_DIM),
        "lru_b_a": 0.02 * jax.random.normal(ks[15], (DEPTH, LRU_HEADS, LRU_HEAD_DIM), f32),
        "lru_a_param": a_param,
        "w_attn_branch": w(ks[16], (DEPTH, ATTN_OUT_WIDTH, D_MODEL), ATTN_OUT_WIDTH),
        "w_rec_branch": w(ks[17], (DEPTH, LRU_WIDTH, D_MODEL), LRU_WIDTH),
        "w_out": w(ks[18], (DEPTH, D_MODEL, D_MODEL), D_MODEL),
        "ffn2_norm_pre": gain(ks[21]),
        "ffn2_norm_post": gain(ks[22]),
        "ffn2_w_gate": w(ks[23], (DEPTH, D_MODEL, D_FF), D_MODEL),
        "ffn2_w_up": w(ks[24], (DEPTH, D_MODEL, D_FF), D_MODEL),
        "ffn2_w_down": w(ks[25], (DEPTH, D_FF, D_MODEL), D_FF),
    }


def reference(x, ffn1_norm_pre, ffn1_norm_post, ffn1_w_gate, ffn1_w_up, ffn1_w_down,
              mix_norm_pre, mix_norm_post, w_in, rel_bias_table, conv_w, conv_b,
              lru_w_x, lru_b_x, lru_w_a, lru_b_a, lru_a_param,
              w_attn_branch, w_rec_branch, w_out,
              ffn2_norm_pre, ffn2_norm_post, ffn2_w_gate, ffn2_w_up, ffn2_w_down):
    B, S, _ = x.shape
    splits = [ATTN_QKV_WIDTH, 2 * ATTN_QKV_WIDTH, 3 * ATTN_QKV_WIDTH,
              3 * ATTN_QKV_WIDTH + LRU_WIDTH, 3 * ATTN_QKV_WIDTH + 2 * LRU_WIDTH]
    h = x
    for l in range(DEPTH):
        f = swiglu(rms_norm(h, ffn1_norm_pre[l]), ffn1_w_gate[l], ffn1_w_up[l], ffn1_w_down[l])
        h = h + 0.5 * rms_norm(f, ffn1_norm_post[l])

        u = rms_norm(h, mix_norm_pre[l])
        proj = u @ w_in[l]
        q, k, v, xr, yr, gl = jnp.split(proj, splits, axis=-1)
        q = q.reshape(B, S, N_GROUPS, HEADS_PER_GROUP, HEAD_DIM)
        k = k.reshape(B, S, N_GROUPS, HEADS_PER_GROUP, HEAD_DIM)
        v = v.reshape(B, S, N_GROUPS, HEADS_PER_GROUP, HEAD_DIM)

        outs, lses = [], []
        for g, (window, dilation) in enumerate(DILATED_GROUPS):
            o_g, lse_g = dilated_group_attention(
                q[:, :, g], k[:, :, g], v[:, :, g],
                rel_bias_table[:, g * HEADS_PER_GROUP:(g + 1) * HEADS_PER_GROUP],
                window, dilation)
            outs.append(o_g)
            lses.append(lse_g)
        alpha = jax.nn.softmax(jnp.stack(lses, axis=0), axis=0)
        attn = jnp.einsum('gbsh,gbshe->bshe', alpha, jnp.stack(outs, axis=0).astype(jnp.float32))
        attn_d = attn.astype(x.dtype).reshape(B, S, ATTN_OUT_WIDTH) @ w_attn_branch[l]

        rec = rg_lru_branch(xr, conv_w[l], conv_b[l], lru_w_x[l], lru_b_x[l],
                            lru_w_a[l], lru_b_a[l], lru_a_param[l]) * jax.nn.gelu(yr)
        rec_d = rec @ w_rec_branch[l]

        gates = jax.nn.sigmoid(gl).reshape(B, S, N_BRANCHES, D_MODEL)
        merged = gates[:, :, 0] * attn_d + gates[:, :, 1] * rec_d
        h = h + rms_norm(merged @ w_out[l], mix_norm_post[l])

        f = swiglu(rms_norm(h, ffn2_norm_pre[l]), ffn2_w_gate[l], ffn2_w_up[l], ffn2_w_down[l])
        h = h + 0.5 * rms_norm(f, ffn2_norm_post[l])
    return h
```

```python
import numpy as np
from contextlib import ExitStack
import concourse.bass as bass
import concourse.mybir as mybir
from concourse.bass_utils import run_bass_kernel_spmd

F32 = mybir.dt.float32
BF16 = mybir.dt.bfloat16
AF = mybir.ActivationFunctionType
ALU = mybir.AluOpType

T = 512
NT = 8
TOK = 4096
DFF = 2816
NJ = 22
EPS = 1e-6
QSCALE = 1.0 / np.sqrt(128.0)
GROUPS = ((128, 1), (512, 4), (2048, 16))
NTAB = (2, 2, 5)
TABOFF = (0, 2, 4)
V_F1PRE, V_F1POST, V_MIXPRE, V_MIXPOST, V_F2PRE, V_F2POST, V_CW0, V_CW1, V_CW2, V_CW3, V_CB, V_BX, V_BA, V_AP = range(14)
NVEC = 14


class Buf:
    __slots__ = ("name", "w", "r")

    def __init__(self, name=""):
        self.name = name
        self.w = None
        self.r = {}


class Sched:
    ENGS = ("pe", "act", "dve", "pool", "sp")

    def __init__(self, nc, stack):
        self.nc = nc
        self.stack = stack
        self.rec = {e: [] for e in self.ENGS}
        self.sems = {}
        self.cnt = {}
        self.seen = {e: {} for e in self.ENGS}
        for e in self.ENGS:
            self.sems[e] = stack.enter_context(nc.semaphore("s_" + e))
            self.cnt[e] = 0
        self.ninst = 0

    def new_sem(self, name):
        self.sems[name] = self.stack.enter_context(self.nc.semaphore(name))
        self.cnt[name] = 0
        return name

    def _wait(self, e, ev):
        if ev is None:
            return
        k, v = ev
        if k == e and e == "pe":
            return
        if k not in self.ENGS:
            v = max(v, self.cnt[k])
        if self.seen[e].get(k, 0) >= v:
            return
        self.seen[e][k] = v
        sem = self.sems[k]
        self.rec[e].append(lambda eng, sem=sem, v=v: eng.wait_ge(sem, v))
        self.ninst += 1

    def _deps(self, e, reads, writes):
        for b in reads:
            self._wait(e, b.w)
        for b in writes:
            self._wait(e, b.w)
            for k, v in b.r.items():
                self._wait(e, (k, v))

    def _mark(self, ev, reads, writes):
        k, v = ev
        for b in reads:
            b.r[k] = v
        for b in writes:
            b.w = ev
            b.r = {}

    def op(self, e, fn, reads=(), writes=()):
        self._deps(e, reads, writes)
        self.cnt[e] += 1
        sem = self.sems[e]
        self.rec[e].append(lambda eng, fn=fn, sem=sem: fn(eng).then_inc(sem, 1))
        self.ninst += 1
        self._mark((e, self.cnt[e]), reads, writes)

    def dma(self, e, semname, out, in_, reads=(), writes=()):
        self._deps(e, reads, writes)
        self.cnt[semname] += 16
        sem = self.sems[semname]
        self.rec[e].append(lambda eng, out=out, in_=in_, sem=sem: eng.dma_start(out=out, in_=in_).then_inc(sem, 16))
        self.ninst += 1
        self._mark((semname, self.cnt[semname]), reads, writes)

    def coll(self, semname, ins, outs, groups, reads=(), writes=()):
        e = "pool"
        self._deps(e, reads, writes)
        self.cnt[semname] += 1
        sem = self.sems[semname]
        self.rec[e].append(lambda eng: eng.collective_compute(
            "AllGather", ALU.bypass, replica_groups=groups, ins=ins, outs=outs).then_inc(sem, 1))
        self._mark((semname, self.cnt[semname]), reads, writes)

    def barrier(self):
        for e in self.ENGS:
            for k in list(self.cnt.keys()):
                if self.cnt[k] > 0:
                    self._wait(e, (k, self.cnt[k]))

    def emit(self):
        nc = self.nc
        rec = self.rec
        with nc.Block() as block:
            @block.tensor
            def _(eng):
                for f in rec["pe"]:
                    f(eng)

            @block.scalar
            def _(eng):
                for f in rec["act"]:
                    f(eng)

            @block.vector
            def _(eng):
                for f in rec["dve"]:
                    f(eng)

            @block.gpsimd
            def _(eng):
                for f in rec["pool"]:
                    f(eng)

            @block.sync
            def _(eng):
                for f in rec["sp"]:
                    f(eng)


class K:
    def __init__(self, nc, stack):
        self.nc = nc
        self.S = Sched(nc, stack)
        self.stack = stack
        self.nm = 0

    def mm(self, out, lhsT, rhs, start, stop, reads=(), writes=(), skip=False):
        if skip:
            self.S.op("pe", lambda e: e.matmul(out, lhsT=lhsT, rhs=rhs, start=start, stop=stop, skip_group_check=True), reads, writes)
        else:
            self.S.op("pe", lambda e: e.matmul(out, lhsT=lhsT, rhs=rhs, start=start, stop=stop), reads, writes)

    def act(self, out, in_, func, reads=(), writes=(), bias=None, scale=None):
        kw = {}
        if bias is not None:
            kw["bias"] = bias
        if scale is not None:
            kw["scale"] = scale
        self.S.op("act", lambda e: e.activation(out=out, in_=in_, func=func, **kw), reads, writes)

    def tt(self, eng, out, in0, in1, op, reads=(), writes=()):
        self.S.op(eng, lambda e: e.tensor_tensor(out=out, in0=in0, in1=in1, op=op), reads, writes)

    def stt(self, eng, out, in0, scalar, in1, op0, op1, reads=(), writes=()):
        self.S.op(eng, lambda e: e.scalar_tensor_tensor(out=out, in0=in0, scalar=scalar, in1=in1, op0=op0, op1=op1), reads, writes)

    def ts(self, eng, out, in0, s1, s2, op0, op1, reads=(), writes=()):
        self.S.op(eng, lambda e: e.tensor_scalar(out=out, in0=in0, scalar1=s1, scalar2=s2, op0=op0, op1=op1), reads, writes)

    def ts1(self, eng, out, in0, s1, op0, reads=(), writes=()):
        self.S.op(eng, lambda e: e.tensor_single_scalar(out=out, in_=in0, scalar=s1, op=op0), reads, writes)

    def cp(self, eng, out, in_, reads=(), writes=()):
        if eng == "act":
            self.S.op("act", lambda e: e.activation(out=out, in_=in_, func=AF.Copy), reads, writes)
        else:
            self.S.op(eng, lambda e: e.tensor_copy(out=out, in_=in_), reads, writes)

    def memset(self, eng, ap, val, writes=()):
        self.S.op(eng, lambda e: e.memset(ap, val), (), writes)

    def recip(self, out, in_, reads=(), writes=()):
        self.S.op("dve", lambda e: e.reciprocal(out=out, in_=in_), reads, writes)

    def scan(self, out, d0, d1, init, reads=(), writes=()):
        self.S.op("dve", lambda e: e.tensor_tensor_scan(out=out, data0=d0, data1=d1, initial=init, op0=ALU.mult, op1=ALU.add), reads, writes)

    def sb(self, st, shape, dt=F32, name=None):
        self.nm += 1
        return st.enter_context(self.nc.sbuf_tensor("%s_%d" % (name or "t", self.nm), shape, dt))

    def sem(self, name):
        self.nm += 1
        return self.S.new_sem("%s_%d" % (name, self.nm))


def build_program():
    nc = bass.Bass("TRN2", target_bir_lowering=False)
    di = lambda name, shape: nc.dram_tensor(name, shape, F32, kind="ExternalInput").ap()
    xT = di("xT", [128, 8, TOK])
    flagd = di("flag", [128, 2])
    wgu_d = [di("wgu1", [128, NJ * 2048]), di("wgu2", [128, NJ * 2048])]
    wd_d = [di("wd1", [128, NJ * 1024]), di("wd2", [128, NJ * 1024])]
    wfm_d = di("wfm", [128, 56 * 1024])
    wv_d = di("wv", [128, 3 * 4096])
    wlx_d = di("wlx", [128, 2048])
    wla_d = di("wla", [128, 2048])
    wab_d = di("wab", [128, 4096])
    wrb_d = di("wrb", [128, 8192])
    wo_d = di("wo", [128, 8192])
    vecs_d = di("vecs", [128, NVEC * 8])
    btab_d = di("btab", [128, 9 * 512])
    mtab_d = di("mtab", [128, 9 * 512])
    yT = nc.dram_tensor("yT", [128, 8, TOK], F32, kind="ExternalOutput").ap()
    import os
    DEBUG = bool(os.environ.get("KDEBUG"))
    dsc = lambda name, shape, dt: (nc.dram_tensor(name, shape, dt, kind="ExternalOutput").ap() if (DEBUG and not name.startswith("wgu"))
                                   else nc.dram_tensor(name, shape, dt).ap())
    wgu_s = [dsc("wgu1s", [128, NJ * 2048], BF16), dsc("wgu2s", [128, NJ * 2048], BF16)]
    h1s = dsc("h1s", [NT, 128, 8 * T], F32)
    h2s = dsc("h2s", [NT, 128, 8 * T], F32)
    u2s = dsc("u2s", [NT, 128, 8 * T], BF16)
    recs = dsc("recs", [NT, 128, 8 * T], BF16)
    agss = dsc("agss", [NT, 128, 8 * T], BF16)
    attns = dsc("attns", [NT, 128, 4 * T], BF16)
    accNs = dsc("accNs", [NT, 128, 4 * T], F32)
    accDs = dsc("accDs", [NT, 128, 4 * T], F32)
    exb_t = [nc.dram_tensor("exb%d" % t_, [128, 8 * T], BF16) for t_ in range(4)]
    gathu_t = [nc.dram_tensor("gathu%d" % t_, [2 * 128, 8 * T], BF16) for t_ in range(4)]
    stb_t = nc.dram_tensor("stb", [128, 8], F32)
    gaths_t = nc.dram_tensor("gaths", [2 * 128, 8], F32)
    exb = [t_.ap() for t_ in exb_t]
    gathu = [t_.ap() for t_ in gathu_t]
    stb, gaths = stb_t.ap(), gaths_t.ap()
    PAIRS = [[0, 1], [2, 3], [4, 5], [6, 7]]

    with ExitStack() as top:
        k = K(nc, top)
        S = k.S
        B_wgu_s = [[Buf() for _ in range(NJ // 2)] for _ in range(2)]
        B_h1s = [Buf() for _ in range(NT)]
        B_h2s = [Buf() for _ in range(NT)]
        B_u2s = [Buf() for _ in range(NT)]
        B_recs = [Buf() for _ in range(NT)]
        B_agss = [Buf() for _ in range(NT)]
        B_attns = [Buf() for _ in range(NT)]
        B_accs = [Buf() for _ in range(NT)]
        B_exb, B_gathu, B_stb, B_gaths, B_y = Buf(), Buf(), Buf(), Buf(), Buf()

        PB = [top.enter_context(nc.psum_tensor("pb%d" % i, [128, 512], F32)) for i in range(8)]
        BPB = [Buf("pb%d" % i) for i in range(8)]

        vecs = k.sb(top, [128, NVEC, 8], F32, "vecs")
        B_vecs = Buf()
        flag = k.sb(top, [128, 2], F32, "flag")
        B_flag = Buf()
        onesM = k.sb(top, [128, 128], BF16, "onesM")
        ones1 = k.sb(top, [128, 128], BF16, "ones1")
        epsc = k.sb(top, [128, 1], F32, "epsc")
        onec = k.sb(top, [128, 1], F32, "onec")
        ghalf = k.sb(top, [128, 2, 8], F32, "ghalf")
        B_const = Buf()
        s_c = k.sem("ldc")
        S.dma("sp", s_c, vecs[:].rearrange("p a b -> p (a b)"), vecs_d, writes=[B_vecs])
        S.dma("sp", s_c, flag[:], flagd, writes=[B_flag])
        k.memset("pool", onesM[:], 1.0 / 1024.0, writes=[B_const])
        k.memset("pool", ones1[:], 1.0, writes=[B_const])
        k.memset("pool", epsc[:], EPS, writes=[B_const])
        k.memset("pool", onec[:], 1.0, writes=[B_const])
        k.ts1("dve", ghalf[:, 0, :], vecs[:, V_F1POST, :], 0.5, ALU.mult, reads=[B_vecs], writes=[B_const])
        k.ts1("dve", ghalf[:, 1, :], vecs[:, V_F2POST, :], 0.5, ALU.mult, reads=[B_vecs], writes=[B_const])

        cast_rr = [0]

        def cast_engine():
            cast_rr[0] += 1
            return ("dve", "pool", "act")[cast_rr[0] % 3]

        PIECE = 2048
        g_stg = [k.sb(top, [128, PIECE], F32, "stg") for _ in range(2)]
        g_Bs = [Buf(), Buf()]
        g_sm = [k.sem("stg"), k.sem("stg")]
        g_n = [0]

        def load_cast(st, dst_ap_fn, src2d, n_elem, Blist, wbuf=None):
            for lo in range(0, n_elem, PIECE):
                hi = min(n_elem, lo + PIECE)
                s = g_n[0] % 2
                g_n[0] += 1
                S.dma("sp", g_sm[s], g_stg[s][:, 0:hi - lo], src2d[:, lo:hi], writes=[g_Bs[s]])
                b = wbuf if wbuf is not None else Buf()
                k.cp(cast_engine(), dst_ap_fn(lo, hi), g_stg[s][:, 0:hi - lo], reads=[g_Bs[s]], writes=[b])
                Blist.append(b)

        def pass_cast_gu():
            with ExitStack() as st:
                piece = 4096
                stg = [k.sb(st, [128, piece], F32, "stg") for _ in range(2)]
                ob = [k.sb(st, [128, piece], BF16, "ob") for _ in range(2)]
                Bs = [Buf(), Buf()]
                Bo = [Buf(), Buf()]
                sm = [k.sem("p0l"), k.sem("p0l")]
                so = [k.sem("p0s"), k.sem("p0s")]
                n = 0
                for f in range(2):
                    for lo in range(0, NJ * 2048, piece):
                        s = n % 2
                        S.dma("sp", sm[s], stg[s][:], wgu_d[f][:, lo:lo + piece], writes=[Bs[s]])
                        k.cp(cast_engine(), ob[s][:], stg[s][:], reads=[Bs[s]], writes=[Bo[s]])
                        S.dma("act", so[s], wgu_s[f][:, lo:lo + piece], ob[s][:], reads=[Bo[s]], writes=[B_wgu_s[f]])
                        n += 1
                S.barrier()

        def rms_stats(Xap, BX, sq, Bsq, bank, Bbank, rs, Brs, rstd, Brstd):
            for c in range(8):
                s = c % 2
                k.act(sq[s][:], Xap(c), AF.Square, reads=[BX[c]], writes=[Bsq[s]])
                k.mm(bank[:], onesM[:], sq[s][:], c == 0, c == 7, reads=[Bsq[s], B_const], writes=[Bbank])
            k.act(rs[:], bank[:], AF.Sqrt, reads=[Bbank, B_const], writes=[Brs], bias=epsc[:], scale=1.0)
            k.recip(rstd[:], rs[:], reads=[Brs], writes=[Brstd])

        def pass_ffn(f, src_fn, Bsrc, src3d, dst_fn, Bdst, dst3d, emit_u2):
            with ExitStack() as st:
                wd = k.sb(st, [128, NJ, 1024], BF16, "wd")
                B_wd = [[] for _ in range(NJ)]
                sGUst = k.sem("stgu")
                X = [k.sb(st, [128, 8, T], F32, "X") for _ in range(2)]
                BX = [[Buf() for _ in range(8)] for _ in range(2)]
                sX = [k.sem("ldx"), k.sem("ldx")]
                sY = [k.sem("sty"), k.sem("sty")]
                U = k.sb(st, [128, 8, T], BF16, "U")
                BU = [Buf() for _ in range(8)]
                U2 = k.sb(st, [128, 8, T], BF16, "U2")
                BU2 = Buf()
                sU2 = k.sem("stu2")
                ACTB = k.sb(st, [128, NJ, T], BF16, "actb")
                BACT = [Buf() for _ in range(NJ)]
                Fb = k.sb(st, [128, 8, T], F32, "F")
                BF = [Buf() for _ in range(8)]
                NSL = 3
                GU = [k.sb(st, [128, 2, 2, 8, 128], BF16, "gu") for _ in range(NSL)]
                BGU = [Buf() for _ in range(NSL)]
                sGU = [k.sem("ldgu") for _ in range(NSL)]
                sg = [k.sb(st, [128, T], F32, "sg") for _ in range(2)]
                Bsg = [Buf(), Buf()]
                sq = [k.sb(st, [128, T], BF16, "sq") for _ in range(2)]
                Bsq = [Buf(), Buf()]
                rs = k.sb(st, [128, T], F32, "rs")
                Brs = Buf()
                rstd = k.sb(st, [128, T], F32, "rstd")
                Brstd = Buf()
                tmp = [k.sb(st, [128, T], F32, "tmp") for _ in range(2)]
                Btmp = [Buf(), Buf()]
                gpre = V_F1PRE if f == 0 else V_F2PRE
                bN, bG, bU, bD = 0, (1, 2), (3, 4), (5, 6)
                guidx = 0
                for i in range(NT):
                    p = i % 2
                    Xi, BXi = X[p], BX[p]
                    S.dma("sp", sX[p], Xi[:] if src3d else Xi[:].rearrange("p a b -> p (a b)"), src_fn(i), reads=[Bsrc[i]], writes=BXi)
                    rms_stats(lambda c: Xi[:, c, :], BXi, sq, Bsq, PB[bN], BPB[bN], rs, Brs, rstd, Brstd)
                    for c in range(8):
                        k.stt("dve", U[:, c, :], Xi[:, c, :], vecs[:, gpre, c:c + 1], rstd[:], ALU.mult, ALU.mult,
                              reads=[BXi[c], Brstd, B_vecs], writes=[BU[c]])
                    for j in range(NJ):
                        sl = (guidx // 2) % NSL
                        if j % 2 == 0 and i == 0 and f == 0:
                            guflat = GU[sl][:].rearrange("p a b c d -> p (a b c d)")
                            for jx in (j, j + 1):
                                load_cast(st, lambda lo, hi, o=(jx - j) * 2048: guflat[:, o + lo:o + hi],
                                          wgu_d[f][:, jx * 2048:(jx + 1) * 2048], 2048, [], wbuf=BGU[sl])
                            S.dma("act", sGUst, wgu_s[f][:, j * 2048:(j + 2) * 2048], guflat,
                                  reads=[BGU[sl]], writes=[B_wgu_s[f][j // 2]])
                        elif j % 2 == 0:
                            S.dma("sp", sGU[sl], GU[sl][:].rearrange("p a b c d -> p (a b c d)"),
                                  wgu_s[f][:, j * 2048:(j + 2) * 2048], reads=[B_wgu_s[f][j // 2]], writes=[BGU[sl]])
                        if i == 0:
                            load_cast(st, lambda lo, hi, jw=j: wd[:, jw, lo:hi], wd_d[f][:, j * 1024:(j + 1) * 1024], 1024, B_wd[j])
                        jj = j % 2
                        guidx += 1
                        g_b, u_b = bG[j % 2], bU[j % 2]
                        for kc in range(8):
                            k.mm(PB[g_b][:], GU[sl][:, jj, 0, kc, :], U[:, kc, :], kc == 0, kc == 7,
                                 reads=[BGU[sl], BU[kc]], writes=[BPB[g_b]])
                        for kc in range(8):
                            k.mm(PB[u_b][:], GU[sl][:, jj, 1, kc, :], U[:, kc, :], kc == 0, kc == 7,
                                 reads=[BGU[sl], BU[kc]], writes=[BPB[u_b]])
                        k.act(sg[j % 2][:], PB[g_b][:], AF.Silu, reads=[BPB[g_b]], writes=[Bsg[j % 2]])
                        k.tt("dve", ACTB[:, j, :], sg[j % 2][:], PB[u_b][:], ALU.mult,
                             reads=[Bsg[j % 2], BPB[u_b]], writes=[BACT[j]])
                    for oc in range(8):
                        d_b = bD[oc % 2]
                        for j in range(NJ):
                            k.mm(PB[d_b][:], wd[:, j, oc * 128:(oc + 1) * 128], ACTB[:, j, :], j == 0, j == NJ - 1,
                                 reads=B_wd[j] + [BACT[j]], writes=[BPB[d_b]])
                        k.cp("act", Fb[:, oc, :], PB[d_b][:], reads=[BPB[d_b]], writes=[BF[oc]])
                    rms_stats(lambda c: Fb[:, c, :], BF, sq, Bsq, PB[bN], BPB[bN], rs, Brs, rstd, Brstd)
                    for c in range(8):
                        s = c % 2
                        k.stt("dve", tmp[s][:], Fb[:, c, :], ghalf[:, f, c:c + 1], rstd[:], ALU.mult, ALU.mult,
                              reads=[BF[c], Brstd, B_const], writes=[Btmp[s]])
                        k.tt("pool", Xi[:, c, :], tmp[s][:], Xi[:, c, :], ALU.add, reads=[Btmp[s]], writes=[BXi[c]])
                    S.dma("act", sY[p], dst_fn(i), Xi[:] if dst3d else Xi[:].rearrange("p a b -> p (a b)"),
                          reads=BXi, writes=[Bdst[i]] if isinstance(Bdst, list) else [Bdst])
                    if emit_u2:
                        rms_stats(lambda c: Xi[:, c, :], BXi, sq, Bsq, PB[bN], BPB[bN], rs, Brs, rstd, Brstd)
                        for c in range(8):
                            k.stt("dve", U2[:, c, :], Xi[:, c, :], vecs[:, V_MIXPRE, c:c + 1], rstd[:], ALU.mult, ALU.mult,
                                  reads=[BXi[c], Brstd, B_vecs], writes=[BU2])
                        S.dma("act", sU2, u2s[i], U2[:].rearrange("p a b -> p (a b)"), reads=[BU2], writes=[B_u2s[i]])
                        if i >= 4:
                            S.dma("act", sU2, exb[i - 4], U2[:].rearrange("p a b -> p (a b)"),
                                  reads=[BU2], writes=[B_exb])
                S.barrier()

        def pass_lru():
            with ExitStack() as st:
                wx = k.sb(st, [128, 16, 8, 128], BF16, "wxr")
                B_wx = []
                wxf = wx[:].rearrange("p a b c -> p (a b c)")
                load_cast(st, lambda lo, hi: wxf[:, lo:hi], wfm_d[:, 24 * 1024:32 * 1024], 8 * 1024, B_wx)
                wl = k.sb(st, [128, 2, 4, 2, 256], BF16, "wl")
                B_wl = []
                load_cast(st, lambda lo, hi: wl[:].rearrange("p a b c d -> p (a b c d)")[:, lo:hi], wlx_d, 2048, B_wl)
                load_cast(st, lambda lo, hi: wl[:].rearrange("p a b c d -> p (a b c d)")[:, 2048 + lo:2048 + hi], wla_d, 2048, B_wl)
                load_cast(st, lambda lo, hi: wxf[:, 8 * 1024 + lo:8 * 1024 + hi], wfm_d[:, 32 * 1024:40 * 1024], 8 * 1024, B_wx)
                c8 = k.sb(st, [128, 8], F32, "c8")
                c16 = k.sb(st, [128, 8], F32, "c16")
                B_c8 = Buf()
                k.act(c8[:], vecs[:, V_AP, :], AF.Exp, reads=[B_vecs], writes=[B_c8], scale=-1.0)
                k.act(c8[:], c8[:], AF.Ln, reads=[B_c8, B_const], writes=[B_c8], bias=onec[:], scale=1.0)
                k.ts1("dve", c16[:], c8[:], -16.0, ALU.mult, reads=[B_c8], writes=[B_c8])
                k.ts1("dve", c8[:], c8[:], -8.0, ALU.mult, reads=[B_c8], writes=[B_c8])
                hcar = k.sb(st, [128, 8], F32, "hcar")
                acar = k.sb(st, [128, 8], F32, "acar")
                B_hc = [Buf() for _ in range(8)]
                B_ac = [Buf() for _ in range(8)]
                xh = k.sb(st, [128, 8, 3], F32, "xh")
                B_xh = [Buf() for _ in range(8)]
                zeros = k.sb(st, [128, T], F32, "zeros")
                B_z = Buf()
                k.memset("pool", zeros[:], 0.0, writes=[B_z])
                k.memset("pool", hcar[:], 0.0, writes=B_hc)
                k.memset("pool", acar[:], 1.0, writes=B_ac)
                U2 = [k.sb(st, [128, 8, T], BF16, "U2") for _ in range(2)]
                BU2 = [Buf(), Buf()]
                sU = [k.sem("ldu"), k.sem("ldu")]
                XRS = [k.sb(st, [128, T + 3], F32, "XRS") for _ in range(4)]
                BXRS = [Buf() for _ in range(4)]
                XCs = [k.sb(st, [128, 4, T], F32, "XC") for _ in range(2)]
                XCBs = [k.sb(st, [128, 4, T], BF16, "XCB") for _ in range(2)]
                GXs = [k.sb(st, [128, 4, T], F32, "GX") for _ in range(2)]
                GAs = [k.sb(st, [128, 4, T], F32, "GA") for _ in range(2)]
                AAs = [k.sb(st, [128, 4, T], F32, "AA") for _ in range(2)]
                BXCs = [[Buf() for _ in range(4)] for _ in range(2)]
                BXCBs = [[Buf() for _ in range(4)] for _ in range(2)]
                BGXs = [[Buf() for _ in range(4)] for _ in range(2)]
                BGAs = [[Buf() for _ in range(4)] for _ in range(2)]
                BAAs = [[Buf() for _ in range(4)] for _ in range(2)]
                GY = [k.sb(st, [128, T], F32, "GY") for _ in range(4)]
                BGY = [Buf() for _ in range(4)]
                REC = [k.sb(st, [128, 8, T], BF16, "REC") for _ in range(2)]
                AGB = [k.sb(st, [128, 8, T], BF16, "AGB") for _ in range(2)]
                BREC = [Buf(), Buf()]
                BAG = [Buf(), Buf()]
                sR = [k.sem("strec"), k.sem("strec")]
                sG = [k.sem("stag"), k.sem("stag")]
                bXR, bGX, bGA, bYR = (0, 1), (2,), (3,), (4, 5, 6, 7)
                S.dma("sp", sU[1], U2[1][:].rearrange("p a b -> p (a b)"), gathu[3][0:128, :],
                      reads=[B_gathu], writes=[BU2[1]])
                for c in range(8):
                    b_ = bXR[c % 2]
                    for kc in range(8):
                        k.mm(PB[b_][:, 0:3], wx[:, c, kc, :], U2[1][:, kc, T - 3:T], kc == 0, kc == 7,
                             reads=[B_wx[c // 2], BU2[1]], writes=[BPB[b_]])
                    k.ts1("dve", xh[:, c, :], PB[b_][:, 0:3], flag[:, 0:1], ALU.mult,
                          reads=[BPB[b_], B_flag], writes=[B_xh[c]])
                units = [(i, hh) for i in range(NT) for hh in range(2)]

                def bufs(u):
                    q_ = u % 2
                    return (XCs[q_], XCBs[q_], GXs[q_], GAs[q_], AAs[q_], BXCs[q_], BXCBs[q_], BGXs[q_], BGAs[q_], BAAs[q_])

                def st_L1(u):
                    i, hh = units[u]
                    p = i % 2
                    Ui = U2[p]
                    if hh == 0:
                        S.dma("sp", sU[p], Ui[:].rearrange("p a b -> p (a b)"), u2s[i], reads=[B_u2s[i]], writes=[BU2[p]])
                    XC, XCB, GX, GA, AA, BXC, BXCB, BGX, BGA, BAA = bufs(u)
                    for cc in range(4):
                        c = 4 * hh + cc
                        b_ = bXR[cc % 2]
                        for kc in range(8):
                            k.mm(PB[b_][:], wx[:, c, kc, :], Ui[:, kc, :], kc == 0, kc == 7,
                                 reads=[B_wx[c // 2], BU2[p]], writes=[BPB[b_]])
                        k.cp("act", XRS[cc][:, 3:T + 3], PB[b_][:], reads=[BPB[b_]], writes=[BXRS[cc]])
                        k.act(XC[:, cc, :], PB[b_][:], AF.Identity, reads=[BPB[b_], B_vecs], writes=[BXC[cc]],
                              bias=vecs[:, V_CB, c:c + 1], scale=vecs[:, V_CW3, c:c + 1])
                    for cc in range(4):
                        c = 4 * hh + cc
                        k.cp("dve", XRS[cc][:, 0:3], xh[:, c, :], reads=[B_xh[c]], writes=[BXRS[cc]])
                    for cc in range(4):
                        c = 4 * hh + cc
                        k.cp("dve", xh[:, c, :], XRS[cc][:, T:T + 3], reads=[BXRS[cc]], writes=[B_xh[c]])
                    for kk in range(3):
                        for cc in range(4):
                            c = 4 * hh + cc
                            k.stt("dve", XC[:, cc, :], XRS[cc][:, kk:T + kk], vecs[:, V_CW0 + kk, c:c + 1], XC[:, cc, :],
                                  ALU.mult, ALU.add, reads=[BXRS[cc], BXC[cc]], writes=[BXC[cc]])
                    for cc in range(4):
                        k.cp("act", XCB[:, cc, :], XC[:, cc, :], reads=[BXC[cc]], writes=[BXCB[cc]])

                def st_L25(u):
                    i, hh = units[u]
                    XC, XCB, GX, GA, AA, BXC, BXCB, BGX, BGA, BAA = bufs(u)
                    for cc in range(4):
                        c = 4 * hh + cc
                        h, jc = c // 2, c % 2
                        l0 = 2 * (cc // 2)
                        gxb, gab = bGX[0], bGA[0]
                        for ic in range(2):
                            k.mm(PB[gxb][:], wl[:, 0, h, ic, jc * 128:(jc + 1) * 128], XCB[:, l0 + ic, :], ic == 0, ic == 1,
                                 reads=B_wl + [BXCB[l0 + ic]], writes=[BPB[gxb]])
                        for ic in range(2):
                            k.mm(PB[gab][:], wl[:, 1, h, ic, jc * 128:(jc + 1) * 128], XCB[:, l0 + ic, :], ic == 0, ic == 1,
                                 reads=B_wl + [BXCB[l0 + ic]], writes=[BPB[gab]])
                        k.act(GX[:, cc, :], PB[gxb][:], AF.Sigmoid, reads=[BPB[gxb], B_vecs], writes=[BGX[cc]],
                              bias=vecs[:, V_BX, c:c + 1], scale=1.0)
                        k.act(GA[:, cc, :], PB[gab][:], AF.Sigmoid, reads=[BPB[gab], B_vecs], writes=[BGA[cc]],
                              bias=vecs[:, V_BA, c:c + 1], scale=1.0)
                    for cc in range(4):
                        c = 4 * hh + cc
                        k.act(AA[:, cc, :], GA[:, cc, :], AF.Exp, reads=[BGA[cc], B_c8], writes=[BAA[cc]], scale=c8[:, c:c + 1])
                    for cc in range(4):
                        c = 4 * hh + cc
                        k.act(GA[:, cc, :], GA[:, cc, :], AF.Exp, reads=[BGA[cc], B_c8], writes=[BGA[cc]], scale=c16[:, c:c + 1])
                    for cc in range(4):
                        k.act(GA[:, cc, :], GA[:, cc, :], AF.Sqrt, reads=[BGA[cc], B_const], writes=[BGA[cc]], bias=onec[:], scale=-1.0)

                def st_L6(u):
                    i, hh = units[u]
                    XC, XCB, GX, GA, AA, BXC, BXCB, BGX, BGA, BAA = bufs(u)
                    for cc in range(4):
                        k.tt("dve", GX[:, cc, :], GX[:, cc, :], XC[:, cc, :], ALU.mult, reads=[BGX[cc], BXC[cc]], writes=[BGX[cc]])
                    for cc in range(4):
                        k.tt("dve", GX[:, cc, :], GX[:, cc, :], GA[:, cc, :], ALU.mult, reads=[BGX[cc], BGA[cc]], writes=[BGX[cc]])
                    for cc in range(4):
                        c = 4 * hh + cc
                        k.scan(XC[:, cc, :], AA[:, cc, :], GX[:, cc, :], hcar[:, c:c + 1],
                               reads=[BAA[cc], BGX[cc], B_hc[c]], writes=[BXC[cc]])
                    for cc in range(4):
                        c = 4 * hh + cc
                        k.scan(GA[:, cc, :], AA[:, cc, :], zeros[:], acar[:, c:c + 1],
                               reads=[BAA[cc], B_z, B_ac[c]], writes=[BGA[cc]])
                    for cc in range(4):
                        c = 4 * hh + cc
                        k.cp("dve", hcar[:, c:c + 1], XC[:, cc, T - 1:T], reads=[BXC[cc]], writes=[B_hc[c]])
                    for cc in range(4):
                        c = 4 * hh + cc
                        k.cp("dve", acar[:, c:c + 1], GA[:, cc, T - 1:T], reads=[BGA[cc]], writes=[B_ac[c]])

                def st_L7a(u):
                    i, hh = units[u]
                    p = i % 2
                    Ui = U2[p]
                    for cc in range(4):
                        c = 4 * hh + cc
                        yb = bYR[cc]
                        for kc in range(8):
                            k.mm(PB[yb][:], wx[:, 8 + c, kc, :], Ui[:, kc, :], kc == 0, kc == 7,
                                 reads=[B_wx[(8 + c) // 2], BU2[p]], writes=[BPB[yb]])
                        k.act(GY[cc][:], PB[yb][:], AF.Gelu, reads=[BPB[yb]], writes=[BGY[cc]])

                def st_L7b(u):
                    i, hh = units[u]
                    p = i % 2
                    XC, XCB, GX, GA, AA, BXC, BXCB, BGX, BGA, BAA = bufs(u)
                    for cc in range(4):
                        c = 4 * hh + cc
                        k.tt("dve", REC[p][:, c, :], XC[:, cc, :], GY[cc][:], ALU.mult,
                             reads=[BXC[cc], BGY[cc]], writes=[BREC[p]])
                        k.tt("dve", AGB[p][:, c, :], GA[:, cc, :], GY[cc][:], ALU.mult,
                             reads=[BGA[cc], BGY[cc]], writes=[BAG[p]])
                    if hh == 1:
                        S.dma("act", sR[p], recs[i], REC[p][:].rearrange("p a b -> p (a b)"), reads=[BREC[p]], writes=[B_recs[i]])
                        S.dma("act", sG[p], agss[i], AGB[p][:].rearrange("p a b -> p (a b)"), reads=[BAG[p]], writes=[B_agss[i]])

                st_L1(0)
                st_L25(0)
                for u in range(len(units)):
                    st_L7a(u)
                    if u + 1 < len(units):
                        st_L1(u + 1)
                    st_L6(u)
                    st_L7b(u)
                    if u + 1 < len(units):
                        st_L25(u + 1)
                sS = k.sem("ststate")
                S.dma("act", sS, stb, hcar[:], reads=B_hc, writes=[B_stb])
                S.barrier()

        def pass_exchange_state():
            S.barrier()
            sc = k.sem("cc")
            S.coll(sc, [stb_t.ap().opt()], [gaths_t.ap().opt()], PAIRS, reads=[B_stb], writes=[B_gaths])
            S._wait("pool", (sc, 1))
            S.barrier()

        def pass_exchange():
            S.barrier()
            for t_ in range(4):
                sc = k.sem("cc")
                S.coll(sc, [exb_t[t_].ap().opt()], [gathu_t[t_].ap().opt()], PAIRS, reads=[B_exb], writes=[B_gathu])
                S._wait("pool", (sc, 1))
            S.barrier()

        def colsel(g, ap3, lead):
            if g == 0:
                return ap3.rearrange(lead + " (m q) -> " + lead + " m q", m=4)
            if g == 1:
                return ap3.rearrange(lead + " (u m) -> " + lead + " m u", m=4)
            return ap3.rearrange(lead + " (u m l) -> " + lead + " m l u", m=4, l=4)

        def unitview(g, ap):
            if g == 2:
                return ap
            return ap

        def pass_attn(g):
            first, last = (g == 0), (g == 2)
            with ExitStack() as st:
                wq = k.sb(st, [128, 4, 8, 128], BF16, "wq")
                wk = k.sb(st, [128, 4, 8, 128], BF16, "wk")
                wv = k.sb(st, [128, 8, 512], BF16, "wv")
                B_w = []
                load_cast(st, lambda lo, hi: wq[:].rearrange("p a b c -> p (a b c)")[:, lo:hi],
                          wfm_d[:, (4 * g) * 1024:(4 * g + 4) * 1024], 4096, B_w)
                load_cast(st, lambda lo, hi: wk[:].rearrange("p a b c -> p (a b c)")[:, lo:hi],
                          wfm_d[:, (12 + 4 * g) * 1024:(12 + 4 * g + 4) * 1024], 4096, B_w)
                load_cast(st, lambda lo, hi: wv[:].rearrange("p a b -> p (a b)")[:, lo:hi],
                          wv_d[:, g * 4096:(g + 1) * 4096], 4096, B_w)
                nt = NTAB[g]
                EB = k.sb(st, [128, nt, 512], F32, "EB")
                B_EB = Buf()
                sT = k.sem("ldtab")
                with ExitStack() as st2:
                    MT = k.sb(st2, [128, nt, 512], F32, "MT")
                    S.dma("sp", sT, EB[:].rearrange("p a b -> p (a b)"), btab_d[:, TABOFF[g] * 512:(TABOFF[g] + nt) * 512], writes=[B_EB])
                    S.dma("sp", sT, MT[:].rearrange("p a b -> p (a b)"), mtab_d[:, TABOFF[g] * 512:(TABOFF[g] + nt) * 512], writes=[B_EB])
                    for t_ in range(nt):
                        k.act(EB[:, t_, :], EB[:, t_, :], AF.Exp, reads=[B_EB], writes=[B_EB])
                        k.tt("dve", EB[:, t_, :], EB[:, t_, :], MT[:, t_, :], ALU.mult, reads=[B_EB], writes=[B_EB])
                    S.barrier()
                NS = 8 if g == 2 else 2
                KT = k.sb(st, [128, NS, 4, 4, 128], BF16, "KT")
                VV = k.sb(st, [128, NS, 4, 512], BF16, "VV")
                BKV = [Buf() for _ in range(NS)]
                Q = k.sb(st, [128, 4, 4, 128], BF16, "Q")
                BQ = Buf()
                U2 = [k.sb(st, [128, 8, T], BF16, "U2") for _ in range(2)]
                BU2 = [Buf(), Buf()]
                sU = [k.sem("ldu"), k.sem("ldu")]
                UP = k.sb(st, [128, 8, T], BF16, "UP") if g > 0 else None
                BUP = Buf()
                NR = 4
                E = [k.sb(st, [128, 512], F32, "E") for _ in range(NR)]
                BE = [Buf() for _ in range(NR)]
                Pb = [k.sb(st, [128, 512], BF16, "Pb") for _ in range(NR)]
                BP = [Buf() for _ in range(NR)]
                accN = [k.sb(st, [128, 4, T], F32, "accN") for _ in range(2)]
                accD = [k.sb(st, [128, 4, T], F32, "accD") for _ in range(2)]
                Bacc = [Buf(), Buf()]
                sA = [k.sem("ldacc"), k.sem("ldacc")]
                sAs = [k.sem("stacc"), k.sem("stacc")]
                ATT = [k.sb(st, [128, 4, T], BF16, "ATT") for _ in range(2)]
                BATT = [Buf(), Buf()]
                bKV, bS, bNm, bDn = (0, 1), (2, 3, 0, 1), (4, 5), (6, 7)
                ctr = {"kv": 0, "s": 0, "u": 0, "ld": 0}

                def slot(i):
                    return i % NS

                def kv_tile(i, Ui, BUi):
                    sl = slot(i)
                    for h in range(4):
                        b_ = bKV[ctr["kv"] % 2]
                        ctr["kv"] += 1
                        for kc in range(8):
                            k.mm(PB[b_][:], wk[:, h, kc, :], Ui[:, kc, :], kc == 0, kc == 7, reads=B_w + [BUi], writes=[BPB[b_]])
                        dst = KT[:, sl, h, :, :]
                        k.cp("act", dst, PB[b_][:].rearrange("e (m q) -> e m q", m=4), reads=[BPB[b_]], writes=[BKV[sl]])
                    for m in range(4):
                        b_ = bKV[ctr["kv"] % 2]
                        ctr["kv"] += 1
                        for kc in range(8):
                            lhs = Ui[:, kc, m * 128:(m + 1) * 128]
                            k.mm(PB[b_][:], lhs, wv[:, kc, :], kc == 0, kc == 7, reads=B_w + [BUi], writes=[BPB[b_]])
                        k.cp("dve", VV[:, sl, m, :], PB[b_][:], reads=[BPB[b_]], writes=[BKV[sl]])

                def permute(Ui, BUi):
                    if g == 0:
                        return Ui, BUi
                    for kc in range(8):
                        dstv = UP[:, kc, :].rearrange("e (m u) -> e m u", m=4) if g == 1 else \
                            UP[:, kc, :].rearrange("e (m l u) -> e m l u", m=4, l=4)
                        k.cp("pool" if kc % 2 else "dve", dstv, colsel(g, Ui[:, kc, :], "e"), reads=[BUi], writes=[BUP])
                    return UP, BUP

                pcs = {"n": 0}
                if g == 0:
                    NPR = 4
                    pstg = [k.sb(st, [128, 2048], F32, "pstg") for _ in range(NPR)]
                    Bpstg = [Buf() for _ in range(NPR)]
                    sPst = [k.sem("ldw2") for _ in range(NPR)]
                    ob = [k.sb(st, [128, 2048], BF16, "ob") for _ in range(NPR)]
                    Bob = [Buf() for _ in range(NPR)]
                    sOb = [k.sem("stw2") for _ in range(NPR)]

                def precast_some(cnt):
                    if g != 0:
                        return
                    for _ in range(cnt):
                        n = pcs["n"]
                        if n >= NJ:
                            return
                        pcs["n"] += 1
                        s_ = n % NPR
                        S.dma("sp", sPst[s_], pstg[s_][:], wgu_d[1][:, n * 2048:(n + 1) * 2048], writes=[Bpstg[s_]])
                        k.cp("pool", ob[s_][:], pstg[s_][:], reads=[Bpstg[s_]], writes=[Bob[s_]])
                        S.dma("act", sOb[s_], wgu_s[1][:, n * 2048:(n + 1) * 2048], ob[s_][:],
                              reads=[Bob[s_]], writes=[B_wgu_s[1][n // 2]])

                halo_tiles = [-1] if g < 2 else [-4, -3, -2, -1]
                for ht in halo_tiles:
                    p = ctr["ld"] % 2
                    ctr["ld"] += 1
                    S.dma("sp", sU[p], U2[p][:].rearrange("p a b -> p (a b)"), gathu[4 + ht][0:128, :],
                          reads=[B_gathu], writes=[BU2[p]])
                    Uh, BUh = permute(U2[p], BU2[p])
                    kv_tile(ht, Uh, BUh)

                for i in range(NT):
                    p = ctr["ld"] % 2
                    ctr["ld"] += 1
                    Ui, BUi = U2[p], BU2[p]
                    S.dma("sp", sU[p], Ui[:].rearrange("p a b -> p (a b)"), u2s[i], reads=[B_u2s[i]], writes=[BUi])
                    pa = i % 2
                    if not first:
                        S.dma("sp", sA[pa], accN[pa][:].rearrange("p a b -> p (a b)"), accNs[i], reads=[B_accs[i]], writes=[Bacc[pa]])
                        S.dma("sp", sA[pa], accD[pa][:].rearrange("p a b -> p (a b)"), accDs[i], reads=[B_accs[i]], writes=[Bacc[pa]])
                    Ui, BUi = permute(Ui, BUi)
                    precast_some(3)
                    kv_tile(i, Ui, BUi)
                    for h in range(4):
                        b_ = bKV[ctr["kv"] % 2]
                        ctr["kv"] += 1
                        for kc in range(8):
                            k.mm(PB[b_][:], wq[:, h, kc, :], Ui[:, kc, :], kc == 0, kc == 7, reads=B_w + [BUi], writes=[BPB[b_]])
                        dst = Q[:, h, :, :]
                        k.cp("act", dst, PB[b_][:].rearrange("e (m q) -> e m q", m=4), reads=[BPB[b_]], writes=[BQ])
                    pairs = []
                    for m in range(4):
                        if g == 0:
                            keys = [(i, m, 0)]
                            keys.append((i, m - 1, 1) if m > 0 else (i - 1, 3, 1))
                        elif g == 1:
                            keys = [(i, m, 0), (i - 1, m, 1)]
                        else:
                            ip = i % 4
                            s0 = i - ip
                            keys = [(s0 + ik, m, ip - ik) for ik in range(ip + 1)]
                            keys += [(s0 - 4 + ik, m, 4 + ip - ik) for ik in range(ip, 4)]
                        nb = bNm[ctr["u"] % 2]
                        db = bDn[ctr["u"] % 2]
                        ctr["u"] += 1
                        for ki, (kt, mk, tab) in enumerate(keys):
                            pairs.append((m, ki, len(keys), kt, mk, tab, nb, db))

                    def emit_S(idx):
                        m, ki, nk, kt, mk, tab, nb, db = pairs[idx]
                        sl = slot(kt)
                        r = ctr["s"] % NR
                        ctr["s"] += 1
                        sb_ = bS[r]
                        for h in range(4):
                            k.mm(PB[sb_][:, h * 128:(h + 1) * 128], KT[:, sl, h, mk, :], Q[:, h, m, :], h == 0, True,
                                 reads=[BKV[sl], BQ], writes=[BPB[sb_]], skip=True)
                        if kt < 0:
                            k.act(E[r][:], PB[sb_][:], AF.Exp, reads=[BPB[sb_], B_flag], writes=[BE[r]],
                                  bias=flag[:, 1:2], scale=float(QSCALE))
                        else:
                            k.act(E[r][:], PB[sb_][:], AF.Exp, reads=[BPB[sb_]], writes=[BE[r]], scale=float(QSCALE))
                        k.tt("dve", Pb[r][:], E[r][:], EB[:, tab, :], ALU.mult, reads=[BE[r], B_EB], writes=[BP[r]])
                        return r

                    def emit_PV(idx, r):
                        m, ki, nk, kt, mk, tab, nb, db = pairs[idx]
                        sl = slot(kt)
                        for h in range(4):
                            k.mm(PB[nb][:, h * 128:(h + 1) * 128], VV[:, sl, mk, h * 128:(h + 1) * 128],
                                 Pb[r][:, h * 128:(h + 1) * 128], ki == 0 and h == 0, ki == nk - 1,
                                 reads=[BKV[sl], BP[r]], writes=[BPB[nb]], skip=True)
                        k.mm(PB[db][:], ones1[:], Pb[r][:], ki == 0, ki == nk - 1,
                             reads=[B_const, BP[r]], writes=[BPB[db]])
                        if ki == nk - 1:
                            nview = PB[nb][:].rearrange("e (h q) -> e h q", h=4)
                            dview = PB[db][:].rearrange("e (h q) -> e h q", h=4)
                            if g == 2:
                                nview = nview.rearrange("e h (l u) -> e h l u", l=4)
                                dview = dview.rearrange("e h (l u) -> e h l u", l=4)
                            oN = colsel(g, accN[pa][:], "e h")[:, :, m]
                            oD = colsel(g, accD[pa][:], "e h")[:, :, m]
                            if first:
                                k.cp("act", oN, nview, reads=[BPB[nb]], writes=[Bacc[pa]])
                                k.cp("dve", oD, dview, reads=[BPB[db]], writes=[Bacc[pa]])
                            else:
                                k.tt("dve", oN, oN, nview, ALU.add, reads=[BPB[nb], Bacc[pa]], writes=[Bacc[pa]])
                                k.tt("dve", oD, oD, dview, ALU.add, reads=[BPB[db], Bacc[pa]], writes=[Bacc[pa]])

                    LA = 2
                    rs_ = {}
                    for idx in range(min(LA, len(pairs))):
                        rs_[idx] = emit_S(idx)
                    for idx in range(len(pairs)):
                        if idx + LA < len(pairs):
                            rs_[idx + LA] = emit_S(idx + LA)
                        emit_PV(idx, rs_[idx])
                    if last:
                        k.recip(accD[pa][:], accD[pa][:], reads=[Bacc[pa]], writes=[Bacc[pa]])
                        k.tt("dve", ATT[pa][:], accN[pa][:], accD[pa][:], ALU.mult, reads=[Bacc[pa]], writes=[BATT[pa]])
                        S.dma("act", sAs[pa], attns[i], ATT[pa][:].rearrange("p a b -> p (a b)"), reads=[BATT[pa]], writes=[B_attns[i]])
                    else:
                        S.dma("act", sAs[pa], accNs[i], accN[pa][:].rearrange("p a b -> p (a b)"), reads=[Bacc[pa]], writes=[B_accs[i]])
                        S.dma("act", sAs[pa], accDs[i], accD[pa][:].rearrange("p a b -> p (a b)"), reads=[Bacc[pa]], writes=[B_accs[i]])
                S.barrier()

        def pass_merge():
            with ExitStack() as st:
                wgl = k.sb(st, [128, 16, 8, 128], BF16, "wgl")
                wab = k.sb(st, [128, 4, 1024], BF16, "wab")
                wrb = k.sb(st, [128, 8, 1024], BF16, "wrb")
                wo = k.sb(st, [128, 8, 1024], BF16, "wo")
                B_wab, B_wrb, B_wo = [], [], []
                B_wgl = [None] * 8
                load_cast(st, lambda lo, hi: wab[:].rearrange("p a b -> p (a b)")[:, lo:hi], wab_d, 4096, B_wab)
                load_cast(st, lambda lo, hi: wrb[:].rearrange("p a b -> p (a b)")[:, lo:hi], wrb_d, 8192, B_wrb)
                wglf = wgl[:].rearrange("p a b c -> p (a b c)")
                for pc in (0, 4, 1, 5, 2, 6, 3, 7):
                    tl = []
                    load_cast(st, lambda lo, hi, o=pc * 2048: wglf[:, o + lo:o + hi],
                              wfm_d[:, 40 * 1024 + pc * 2048:40 * 1024 + (pc + 1) * 2048], 2048, tl)
                    B_wgl[pc] = tl[0]
                load_cast(st, lambda lo, hi: wo[:].rearrange("p a b -> p (a b)")[:, lo:hi], wo_d, 8192, B_wo)
                H = [k.sb(st, [128, 8, T], F32, "H") for _ in range(2)]
                BH = [[Buf() for _ in range(8)] for _ in range(2)]
                U2 = [k.sb(st, [128, 8, T], BF16, "U2") for _ in range(2)]
                AT = [k.sb(st, [128, 4, T], BF16, "AT") for _ in range(2)]
                RC = k.sb(st, [128, 8, T], BF16, "RC")
                AGm = k.sb(st, [128, 8, T], BF16, "AGm")
                BRC = [Buf() for _ in range(8)]
                BAGm = Buf()
                sRC = k.sem("ldrc")
                hinit = k.sb(st, [128, 8], F32, "hinit")
                B_hi = Buf()
                sHi = k.sem("ldhi")
                S.dma("sp", sHi, hinit[:], gaths[0:128, :], reads=[B_gaths], writes=[B_hi])
                k.ts1("dve", hinit[:], hinit[:], flag[:, 0:1], ALU.mult, reads=[B_hi, B_flag], writes=[B_hi])
                BIN = [Buf(), Buf()]
                sIn = [k.sem("ldin"), k.sem("ldin")]
                sO = [k.sem("sth2"), k.sem("sth2")]
                MRG = k.sb(st, [128, 8, T], BF16, "MRG")
                BM = [Buf() for _ in range(8)]
                Yf = k.sb(st, [128, 8, T], F32, "Yf")
                BY = [Buf() for _ in range(8)]
                s0 = [k.sb(st, [128, T], F32, "s0")] * 2
                s1 = [k.sb(st, [128, T], F32, "s1")] * 2
                m1 = [k.sb(st, [128, T], F32, "m1")] * 2
                m2 = [k.sb(st, [128, T], F32, "m2")] * 2
                Bs0, Bs1, Bm1, Bm2 = [Buf()] * 2, [Buf()] * 2, [Buf()] * 2, [Buf()] * 2
                sq = [k.sb(st, [128, T], BF16, "sq") for _ in range(2)]
                Bsq = [Buf(), Buf()]
                rs = k.sb(st, [128, T], F32, "rs")
                Brs = Buf()
                rstd = k.sb(st, [128, T], F32, "rstd")
                Brstd = Buf()
                bG0s, bG1s, bADs, bRDs, bY, bN = (0, 1), (2, 3), (4, 5), (6, 7), (0, 1), 2
                for i in range(NT):
                    p = i % 2
                    S.dma("sp", sIn[p], H[p][:].rearrange("p a b -> p (a b)"), h1s[i], reads=[B_h1s[i]], writes=BH[p])
                    S.dma("sp", sIn[p], U2[p][:].rearrange("p a b -> p (a b)"), u2s[i], reads=[B_u2s[i]], writes=[BIN[p]])
                    S.dma("sp", sIn[p], AT[p][:].rearrange("p a b -> p (a b)"), attns[i], reads=[B_attns[i]], writes=[BIN[p]])
                    S.dma("sp", sRC, RC[:].rearrange("p a b -> p (a b)"), recs[i], reads=[B_recs[i]], writes=BRC)
                    S.dma("sp", sRC, AGm[:].rearrange("p a b -> p (a b)"), agss[i], reads=[B_agss[i]], writes=[BAGm])
                    for c in range(8):
                        k.stt("dve", RC[:, c, :], AGm[:, c, :], hinit[:, c:c + 1], RC[:, c, :], ALU.mult, ALU.add,
                              reads=[BAGm, BRC[c], B_hi], writes=[BRC[c]])
                    for c in range(8):
                        s = c % 2
                        bG0, bG1, bAD, bRD = bG0s[s], bG1s[s], bADs[s], bRDs[s]
                        for kc in range(8):
                            k.mm(PB[bG0][:], wgl[:, c, kc, :], U2[p][:, kc, :], kc == 0, kc == 7, reads=[B_wgl[c // 2], BIN[p]], writes=[BPB[bG0]])
                        for kc in range(8):
                            k.mm(PB[bG1][:], wgl[:, 8 + c, kc, :], U2[p][:, kc, :], kc == 0, kc == 7, reads=[B_wgl[(8 + c) // 2], BIN[p]], writes=[BPB[bG1]])
                        for kc in range(4):
                            k.mm(PB[bAD][:], wab[:, kc, c * 128:(c + 1) * 128], AT[p][:, kc, :], kc == 0, kc == 3, reads=B_wab + [BIN[p]], writes=[BPB[bAD]])
                        for kc in range(8):
                            k.mm(PB[bRD][:], wrb[:, kc, c * 128:(c + 1) * 128], RC[:, kc, :], kc == 0, kc == 7, reads=B_wrb + [BRC[kc]], writes=[BPB[bRD]])
                        k.act(s0[s][:], PB[bG0][:], AF.Sigmoid, reads=[BPB[bG0]], writes=[Bs0[s]])
                        k.act(s1[s][:], PB[bG1][:], AF.Sigmoid, reads=[BPB[bG1]], writes=[Bs1[s]])
                        k.tt("dve", m1[s][:], s0[s][:], PB[bAD][:], ALU.mult, reads=[Bs0[s], BPB[bAD]], writes=[Bm1[s]])
                        k.tt("dve", m2[s][:], s1[s][:], PB[bRD][:], ALU.mult, reads=[Bs1[s], BPB[bRD]], writes=[Bm2[s]])
                        k.tt("pool", MRG[:, c, :], m1[s][:], m2[s][:], ALU.add, reads=[Bm1[s], Bm2[s]], writes=[BM[c]])
                    for oc in range(8):
                        yb = bY[oc % 2]
                        for kc in range(8):
                            k.mm(PB[yb][:], wo[:, kc, oc * 128:(oc + 1) * 128], MRG[:, kc, :], kc == 0, kc == 7, reads=B_wo + [BM[kc]], writes=[BPB[yb]])
                        k.cp("act", Yf[:, oc, :], PB[yb][:], reads=[BPB[yb]], writes=[BY[oc]])
                    rms_stats(lambda c: Yf[:, c, :], BY, sq, Bsq, PB[bN], BPB[bN], rs, Brs, rstd, Brstd)
                    for c in range(8):
                        s = c % 2
                        k.stt("dve", m1[s][:], Yf[:, c, :], vecs[:, V_MIXPOST, c:c + 1], rstd[:], ALU.mult, ALU.mult,
                              reads=[BY[c], Brstd, B_vecs], writes=[Bm1[s]])
                        k.tt("pool", H[p][:, c, :], m1[s][:], H[p][:, c, :], ALU.add, reads=[Bm1[s]], writes=[BH[p][c]])
                    S.dma("act", sO[p], h2s[i], H[p][:].rearrange("p a b -> p (a b)"), reads=BH[p], writes=[B_h2s[i]])
                S.barrier()

        pass_ffn(0, lambda i: xT[:, :, i * T:(i + 1) * T], [Buf() for _ in range(NT)], True, lambda i: h1s[i], B_h1s, False, True)
        pass_exchange()
        pass_lru()
        for g in range(3):
            pass_attn(g)
        pass_exchange_state()
        pass_merge()
        pass_ffn(1, lambda i: h2s[i], B_h2s, False, lambda i: yT[:, :, i * T:(i + 1) * T], B_y, True, False)
        S.barrier()
        print("instructions:", S.ninst, {e: len(S.rec[e]) for e in S.ENGS})
        S.emit()
    return nc


def _t5_bucket(dist):
    import math
    max_exact = 16
    nf = np.maximum(dist, 1).astype(np.float32)
    large = max_exact + (np.log(nf / np.float32(max_exact)) / np.float32(math.log(2048 / max_exact))
                         * np.float32(32 - max_exact)).astype(np.int32)
    large = np.minimum(large, 31)
    return np.where(dist < max_exact, dist, large)


def _tables(rel_bias_table):
    bt = np.zeros((9, 128, 4, 128), np.float32)
    mt = np.zeros((9, 128, 4, 128), np.float32)
    kk = np.arange(128)[:, None]
    qq = np.arange(128)[None, :]
    for g in range(2):
        d = GROUPS[g][1]
        for kb in range(2):
            steps = qq - kk + (128 if kb == 1 else 0)
            valid = (steps >= 0) & (steps <= 128)
            bucket = _t5_bucket(np.maximum(steps, 0) * d)
            for h in range(4):
                bt[2 * g + kb, :, h, :] = np.where(valid, rel_bias_table[bucket, g * 4 + h], 0.0)
                mt[2 * g + kb, :, h, :] = valid
    d = 16
    lk = (np.arange(128) // 32)[:, None]
    uk = (np.arange(128) % 32)[:, None]
    lq = (np.arange(128) // 32)[None, :]
    uq = (np.arange(128) % 32)[None, :]
    for dl in range(5):
        steps = 32 * dl + uq - uk
        valid = (steps >= 0) & (steps <= 128) & (lk == lq)
        bucket = _t5_bucket(np.maximum(steps, 0) * d)
        for h in range(4):
            bt[4 + dl, :, h, :] = np.where(valid, rel_bias_table[bucket, 8 + h], 0.0)
            mt[4 + dl, :, h, :] = valid
    return (np.ascontiguousarray(bt.transpose(1, 0, 2, 3).reshape(128, 9 * 512)),
            np.ascontiguousarray(mt.transpose(1, 0, 2, 3).reshape(128, 9 * 512)))


def _fm(w):
    C = w.shape[1]
    return w.reshape(8, 128, C // 128, 128).transpose(1, 2, 0, 3)


def _prep_weights(inp):
    f32 = np.float32
    out = {}
    for f, nm in ((1, "ffn1"), (2, "ffn2")):
        g = _fm(inp[nm + "_w_gate"][0])
        u = _fm(inp[nm + "_w_up"][0])
        gu = np.stack([g, u], axis=2)
        out["wgu%d" % f] = np.ascontiguousarray(gu.reshape(128, NJ * 2048), dtype=f32)
        wd = inp[nm + "_w_down"][0].reshape(NJ, 128, 1024).transpose(1, 0, 2)
        out["wd%d" % f] = np.ascontiguousarray(wd.reshape(128, NJ * 1024), dtype=f32)
    w_in = inp["w_in"][0]
    q, kk, v = w_in[:, 0:1536], w_in[:, 1536:3072], w_in[:, 3072:4608]
    rest = w_in[:, 4608:]
    fm = np.concatenate([_fm(q), _fm(kk), _fm(rest)], axis=1)
    out["wfm"] = np.ascontiguousarray(fm.reshape(128, 56 * 1024), dtype=f32)
    wv = v.reshape(8, 128, 3, 512).transpose(1, 2, 0, 3)
    out["wv"] = np.ascontiguousarray(wv.reshape(128, 3 * 4096), dtype=f32)
    for nm, key in (("wlx", "lru_w_x"), ("wla", "lru_w_a")):
        w = inp[key][0].reshape(4, 2, 128, 256).transpose(2, 0, 1, 3)
        out[nm] = np.ascontiguousarray(w.reshape(128, 2048), dtype=f32)
    out["wab"] = np.ascontiguousarray(inp["w_attn_branch"][0].reshape(4, 128, 1024).transpose(1, 0, 2).reshape(128, 4096), dtype=f32)
    out["wrb"] = np.ascontiguousarray(inp["w_rec_branch"][0].reshape(8, 128, 1024).transpose(1, 0, 2).reshape(128, 8192), dtype=f32)
    out["wo"] = np.ascontiguousarray(inp["w_out"][0].reshape(8, 128, 1024).transpose(1, 0, 2).reshape(128, 8192), dtype=f32)
    vl = [inp["ffn1_norm_pre"][0], inp["ffn1_norm_post"][0], inp["mix_norm_pre"][0], inp["mix_norm_post"][0],
          inp["ffn2_norm_pre"][0], inp["ffn2_norm_post"][0],
          inp["conv_w"][0][0], inp["conv_w"][0][1], inp["conv_w"][0][2], inp["conv_w"][0][3],
          inp["conv_b"][0], inp["lru_b_x"][0].reshape(-1), inp["lru_b_a"][0].reshape(-1), inp["lru_a_param"][0]]
    vecs = np.stack([np.asarray(v_, f32).reshape(8, 128).T for v_ in vl], axis=1)
    out["vecs"] = np.ascontiguousarray(vecs.reshape(128, NVEC * 8), dtype=f32)
    bt, mt = _tables(np.asarray(inp["rel_bias_table"], f32))
    out["btab"], out["mtab"] = bt, mt
    return out


_CACHE = {}


def kernel(**inputs):
    inp = {k_: np.asarray(v_) for k_, v_ in inputs.items()}
    x = inp["x"].astype(np.float32, copy=False)
    if "nc" not in _CACHE:
        _CACHE["nc"] = build_program()
    nc = _CACHE["nc"]
    wts = _prep_weights(inp)
    in_maps = []
    for c in range(8):
        b, half = c // 2, c % 2
        xs = x[b, half * TOK:(half + 1) * TOK, :]
        xTc = np.ascontiguousarray(xs.reshape(TOK, 8, 128).transpose(2, 1, 0))
        fl = np.zeros((128, 2), np.float32)
        fl[:, 0] = float(half)
        fl[:, 1] = (float(half) - 1.0) * 30000.0
        m = {"xT": xTc, "flag": fl}
        m.update(wts)
        in_maps.append(m)
    res = run_bass_kernel_spmd(nc, in_maps, core_ids=list(range(8)))
    _CACHE["res"] = res
    out = np.empty((4, 2 * TOK, 1024), np.float32)
    for c in range(8):
        b, half = c // 2, c % 2
        yT = np.asarray(res.results[c]["yT"])
        out[b, half * TOK:(half + 1) * TOK, :] = yT.transpose(2, 1, 0).reshape(TOK, 1024)
    return out
```

```python
import numpy as np
from contextlib import ExitStack
import concourse.bass as bass
import concourse.mybir as mybir
from concourse.bass_utils import run_bass_kernel_spmd

F32 = mybir.dt.float32
BF16 = mybir.dt.bfloat16
AF = mybir.ActivationFunctionType
ALU = mybir.AluOpType

T = 512
NT = 8
TOK = 4096
DFF = 2816
NJ = 22
EPS = 1e-6
QSCALE = 1.0 / np.sqrt(128.0)
GROUPS = ((128, 1), (512, 4), (2048, 16))
NTAB = (2, 2, 5)
TABOFF = (0, 2, 4)
V_F1PRE, V_F1POST, V_MIXPRE, V_MIXPOST, V_F2PRE, V_F2POST, V_CW0, V_CW1, V_CW2, V_CW3, V_CB, V_BX, V_BA, V_AP = range(14)
NVEC = 14


class Buf:
    __slots__ = ("name", "w", "r")

    def __init__(self, name=""):
        self.name = name
        self.w = None
        self.r = {}


class Sched:
    ENGS = ("pe", "act", "dve", "pool", "sp")

    def __init__(self, nc, stack):
        self.nc = nc
        self.stack = stack
        self.rec = {e: [] for e in self.ENGS}
        self.sems = {}
        self.cnt = {}
        self.seen = {e: {} for e in self.ENGS}
        for e in self.ENGS:
            self.sems[e] = stack.enter_context(nc.semaphore("s_" + e))
            self.cnt[e] = 0
        self.ninst = 0

    def new_sem(self, name):
        self.sems[name] = self.stack.enter_context(self.nc.semaphore(name))
        self.cnt[name] = 0
        return name

    def _wait(self, e, ev):
        if ev is None:
            return
        k, v = ev
        if k == e and e == "pe":
            return
        if k not in self.ENGS:
            v = max(v, self.cnt[k])
        if self.seen[e].get(k, 0) >= v:
            return
        self.seen[e][k] = v
        sem = self.sems[k]
        self.rec[e].append(lambda eng, sem=sem, v=v: eng.wait_ge(sem, v))
        self.ninst += 1

    def _deps(self, e, reads, writes):
        for b in reads:
            self._wait(e, b.w)
        for b in writes:
            self._wait(e, b.w)
            for k, v in b.r.items():
                self._wait(e, (k, v))

    def _mark(self, ev, reads, writes):
        k, v = ev
        for b in reads:
            b.r[k] = v
        for b in writes:
            b.w = ev
            b.r = {}

    def op(self, e, fn, reads=(), writes=()):
        self._deps(e, reads, writes)
        self.cnt[e] += 1
        sem = self.sems[e]
        self.rec[e].append(lambda eng, fn=fn, sem=sem: fn(eng).then_inc(sem, 1))
        self.ninst += 1
        self._mark((e, self.cnt[e]), reads, writes)

    def dma(self, e, semname, out, in_, reads=(), writes=()):
        self._deps(e, reads, writes)
        self.cnt[semname] += 16
        sem = self.sems[semname]
        self.rec[e].append(lambda eng, out=out, in_=in_, sem=sem: eng.dma_start(out=out, in_=in_).then_inc(sem, 16))
        self.ninst += 1
        self._mark((semname, self.cnt[semname]), reads, writes)

    def coll(self, semname, ins, outs, groups, reads=(), writes=()):
        e = "pool"
        self._deps(e, reads, writes)
        self.cnt[semname] += 1
        sem = self.sems[semname]
        self.rec[e].append(lambda eng: eng.collective_compute(
            "AllGather", ALU.bypass, replica_groups=groups, ins=ins, outs=outs).then_inc(sem, 1))
        self._mark((semname, self.cnt[semname]), reads, writes)

    def barrier(self):
        for e in self.ENGS:
            for k in list(self.cnt.keys()):
                if self.cnt[k] > 0:
                    self._wait(e, (k, self.cnt[k]))

    def emit(self):
        nc = self.nc
        rec = self.rec
        with nc.Block() as block:
            @block.tensor
            def _(eng):
                for f in rec["pe"]:
                    f(eng)

            @block.scalar
            def _(eng):
                for f in rec["act"]:
                    f(eng)

            @block.vector
            def _(eng):
                for f in rec["dve"]:
                    f(eng)

            @block.gpsimd
            def _(eng):
                for f in rec["pool"]:
                    f(eng)

            @block.sync
            def _(eng):
                for f in rec["sp"]:
                    f(eng)


class K:
    def __init__(self, nc, stack):
        self.nc = nc
        self.S = Sched(nc, stack)
        self.stack = stack
        self.nm = 0

    def mm(self, out, lhsT, rhs, start, stop, reads=(), writes=(), skip=False):
        if skip:
            self.S.op("pe", lambda e: e.matmul(out, lhsT=lhsT, rhs=rhs, start=start, stop=stop, skip_group_check=True), reads, writes)
        else:
            self.S.op("pe", lambda e: e.matmul(out, lhsT=lhsT, rhs=rhs, start=start, stop=stop), reads, writes)

    def act(self, out, in_, func, reads=(), writes=(), bias=None, scale=None):
        kw = {}
        if bias is not None:
            kw["bias"] = bias
        if scale is not None:
            kw["scale"] = scale
        self.S.op("act", lambda e: e.activation(out=out, in_=in_, func=func, **kw), reads, writes)

    def tt(self, eng, out, in0, in1, op, reads=(), writes=()):
        self.S.op(eng, lambda e: e.tensor_tensor(out=out, in0=in0, in1=in1, op=op), reads, writes)

    def stt(self, eng, out, in0, scalar, in1, op0, op1, reads=(), writes=()):
        self.S.op(eng, lambda e: e.scalar_tensor_tensor(out=out, in0=in0, scalar=scalar, in1=in1, op0=op0, op1=op1), reads, writes)

    def ts(self, eng, out, in0, s1, s2, op0, op1, reads=(), writes=()):
        self.S.op(eng, lambda e: e.tensor_scalar(out=out, in0=in0, scalar1=s1, scalar2=s2, op0=op0, op1=op1), reads, writes)

    def ts1(self, eng, out, in0, s1, op0, reads=(), writes=()):
        self.S.op(eng, lambda e: e.tensor_single_scalar(out=out, in_=in0, scalar=s1, op=op0), reads, writes)

    def cp(self, eng, out, in_, reads=(), writes=()):
        if eng == "act":
            self.S.op("act", lambda e: e.activation(out=out, in_=in_, func=AF.Copy), reads, writes)
        else:
            self.S.op(eng, lambda e: e.tensor_copy(out=out, in_=in_), reads, writes)

    def memset(self, eng, ap, val, writes=()):
        self.S.op(eng, lambda e: e.memset(ap, val), (), writes)

    def recip(self, out, in_, reads=(), writes=()):
        self.S.op("dve", lambda e: e.reciprocal(out=out, in_=in_), reads, writes)

    def scan(self, out, d0, d1, init, reads=(), writes=()):
        self.S.op("dve", lambda e: e.tensor_tensor_scan(out=out, data0=d0, data1=d1, initial=init, op0=ALU.mult, op1=ALU.add), reads, writes)

    def sb(self, st, shape, dt=F32, name=None):
        self.nm += 1
        return st.enter_context(self.nc.sbuf_tensor("%s_%d" % (name or "t", self.nm), shape, dt))

    def sem(self, name):
        self.nm += 1
        return self.S.new_sem("%s_%d" % (name, self.nm))


def build_program():
    nc = bass.Bass("TRN2", target_bir_lowering=False)
    di = lambda name, shape: nc.dram_tensor(name, shape, F32, kind="ExternalInput").ap()
    xT = di("xT", [128, 8, TOK])
    flagd = di("flag", [128, 2])
    wgu_d = [di("wgu1", [128, NJ * 2048]), di("wgu2", [128, NJ * 2048])]
    wd_d = [di("wd1", [128, NJ * 1024]), di("wd2", [128, NJ * 1024])]
    wfm_d = di("wfm", [128, 56 * 1024])
    wv_d = di("wv", [128, 3 * 4096])
    wlx_d = di("wlx", [128, 2048])
    wla_d = di("wla", [128, 2048])
    wab_d = di("wab", [128, 4096])
    wrb_d = di("wrb", [128, 8192])
    wo_d = di("wo", [128, 8192])
    vecs_d = di("vecs", [128, NVEC * 8])
    btab_d = di("btab", [128, 9 * 512])
    mtab_d = di("mtab", [128, 9 * 512])
    yT = nc.dram_tensor("yT", [128, 8, TOK], F32, kind="ExternalOutput").ap()
    import os
    DEBUG = bool(os.environ.get("KDEBUG"))
    dsc = lambda name, shape, dt: (nc.dram_tensor(name, shape, dt, kind="ExternalOutput").ap() if (DEBUG and not name.startswith("wgu"))
                                   else nc.dram_tensor(name, shape, dt).ap())
    wgu_s = [dsc("wgu1s", [128, NJ * 2048], BF16), dsc("wgu2s", [128, NJ * 2048], BF16)]
    h1s = dsc("h1s", [NT, 128, 8 * T], F32)
    h2s = dsc("h2s", [NT, 128, 8 * T], F32)
    u2s = dsc("u2s", [NT, 128, 8 * T], BF16)
    recs = dsc("recs", [NT, 128, 8 * T], BF16)
    agss = dsc("agss", [NT, 128, 8 * T], BF16)
    attns = dsc("attns", [NT, 128, 4 * T], BF16)
    accNs = dsc("accNs", [NT, 128, 4 * T], F32)
    accDs = dsc("accDs", [NT, 128, 4 * T], F32)
    exb_t = [nc.dram_tensor("exb%d" % t_, [128, 8 * T], BF16) for t_ in range(4)]
    gathu_t = [nc.dram_tensor("gathu%d" % t_, [2 * 128, 8 * T], BF16) for t_ in range(4)]
    stb_t = nc.dram_tensor("stb", [128, 8], F32)
    gaths_t = nc.dram_tensor("gaths", [2 * 128, 8], F32)
    exb = [t_.ap() for t_ in exb_t]
    gathu = [t_.ap() for t_ in gathu_t]
    stb, gaths = stb_t.ap(), gaths_t.ap()
    PAIRS = [[0, 1], [2, 3], [4, 5], [6, 7]]

    with ExitStack() as top:
        k = K(nc, top)
        S = k.S
        B_wgu_s = [[Buf() for _ in range(NJ // 2)] for _ in range(2)]
        B_h1s = [Buf() for _ in range(NT)]
        B_h2s = [Buf() for _ in range(NT)]
        B_u2s = [Buf() for _ in range(NT)]
        B_recs = [Buf() for _ in range(NT)]
        B_agss = [Buf() for _ in range(NT)]
        B_attns = [Buf() for _ in range(NT)]
        B_accs = [Buf() for _ in range(NT)]
        B_exb, B_gathu, B_stb, B_gaths, B_y = Buf(), Buf(), Buf(), Buf(), Buf()

        PB = [top.enter_context(nc.psum_tensor("pb%d" % i, [128, 512], F32)) for i in range(8)]
        BPB = [Buf("pb%d" % i) for i in range(8)]

        vecs = k.sb(top, [128, NVEC, 8], F32, "vecs")
        B_vecs = Buf()
        flag = k.sb(top, [128, 2], F32, "flag")
        B_flag = Buf()
        onesM = k.sb(top, [128, 128], BF16, "onesM")
        ones1 = k.sb(top, [128, 128], BF16, "ones1")
        epsc = k.sb(top, [128, 1], F32, "epsc")
        onec = k.sb(top, [128, 1], F32, "onec")
        ghalf = k.sb(top, [128, 2, 8], F32, "ghalf")
        B_const = Buf()
        s_c = k.sem("ldc")
        S.dma("sp", s_c, vecs[:].rearrange("p a b -> p (a b)"), vecs_d, writes=[B_vecs])
        S.dma("sp", s_c, flag[:], flagd, writes=[B_flag])
        k.memset("pool", onesM[:], 1.0 / 1024.0, writes=[B_const])
        k.memset("pool", ones1[:], 1.0, writes=[B_const])
        k.memset("pool", epsc[:], EPS, writes=[B_const])
        k.memset("pool", onec[:], 1.0, writes=[B_const])
        k.ts1("dve", ghalf[:, 0, :], vecs[:, V_F1POST, :], 0.5, ALU.mult, reads=[B_vecs], writes=[B_const])
        k.ts1("dve", ghalf[:, 1, :], vecs[:, V_F2POST, :], 0.5, ALU.mult, reads=[B_vecs], writes=[B_const])

        cast_rr = [0]

        def cast_engine():
            cast_rr[0] += 1
            return ("dve", "pool", "act")[cast_rr[0] % 3]

        PIECE = 2048
        g_stg = [k.sb(top, [128, PIECE], F32, "stg") for _ in range(2)]
        g_Bs = [Buf(), Buf()]
        g_sm = [k.sem("stg"), k.sem("stg")]
        g_n = [0]

        def load_cast(st, dst_ap_fn, src2d, n_elem, Blist, wbuf=None):
            for lo in range(0, n_elem, PIECE):
                hi = min(n_elem, lo + PIECE)
                s = g_n[0] % 2
                g_n[0] += 1
                S.dma("sp", g_sm[s], g_stg[s][:, 0:hi - lo], src2d[:, lo:hi], writes=[g_Bs[s]])
                b = wbuf if wbuf is not None else Buf()
                k.cp(cast_engine(), dst_ap_fn(lo, hi), g_stg[s][:, 0:hi - lo], reads=[g_Bs[s]], writes=[b])
                Blist.append(b)

        def pass_cast_gu():
            with ExitStack() as st:
                piece = 4096
                stg = [k.sb(st, [128, piece], F32, "stg") for _ in range(2)]
                ob = [k.sb(st, [128, piece], BF16, "ob") for _ in range(2)]
                Bs = [Buf(), Buf()]
                Bo = [Buf(), Buf()]
                sm = [k.sem("p0l"), k.sem("p0l")]
                so = [k.sem("p0s"), k.sem("p0s")]
                n = 0
                for f in range(2):
                    for lo in range(0, NJ * 2048, piece):
                        s = n % 2
                        S.dma("sp", sm[s], stg[s][:], wgu_d[f][:, lo:lo + piece], writes=[Bs[s]])
                        k.cp(cast_engine(), ob[s][:], stg[s][:], reads=[Bs[s]], writes=[Bo[s]])
                        S.dma("act", so[s], wgu_s[f][:, lo:lo + piece], ob[s][:], reads=[Bo[s]], writes=[B_wgu_s[f]])
                        n += 1
                S.barrier()

        def rms_stats(Xap, BX, sq, Bsq, bank, Bbank, rs, Brs, rstd, Brstd):
            for c in range(8):
                s = c % 2
                k.act(sq[s][:], Xap(c), AF.Square, reads=[BX[c]], writes=[Bsq[s]])
                k.mm(bank[:], onesM[:], sq[s][:], c == 0, c == 7, reads=[Bsq[s], B_const], writes=[Bbank])
            k.act(rs[:], bank[:], AF.Sqrt, reads=[Bbank, B_const], writes=[Brs], bias=epsc[:], scale=1.0)
            k.recip(rstd[:], rs[:], reads=[Brs], writes=[Brstd])

        def pass_ffn(f, src_fn, Bsrc, src3d, dst_fn, Bdst, dst3d, emit_u2):
            with ExitStack() as st:
                wd = k.sb(st, [128, NJ, 1024], BF16, "wd")
                B_wd = [[] for _ in range(NJ)]
                sGUst = k.sem("stgu")
                X = [k.sb(st, [128, 8, T], F32, "X") for _ in range(2)]
                BX = [[Buf() for _ in range(8)] for _ in range(2)]
                sX = [k.sem("ldx"), k.sem("ldx")]
                sY = [k.sem("sty"), k.sem("sty")]
                U = k.sb(st, [128, 8, T], BF16, "U")
                BU = [Buf() for _ in range(8)]
                U2 = k.sb(st, [128, 8, T], BF16, "U2")
                BU2 = Buf()
                sU2 = k.sem("stu2")
                ACTB = k.sb(st, [128, NJ, T], BF16, "actb")
                BACT = [Buf() for _ in range(NJ)]
                Fb = k.sb(st, [128, 8, T], F32, "F")
                BF = [Buf() for _ in range(8)]
                NSL = 3
                GU = [k.sb(st, [128, 2, 2, 8, 128], BF16, "gu") for _ in range(NSL)]
                BGU = [Buf() for _ in range(NSL)]
                sGU = [k.sem("ldgu") for _ in range(NSL)]
                sg = [k.sb(st, [128, T], F32, "sg") for _ in range(2)]
                Bsg = [Buf(), Buf()]
                sq = [k.sb(st, [128, T], BF16, "sq") for _ in range(2)]
                Bsq = [Buf(), Buf()]
                rs = k.sb(st, [128, T], F32, "rs")
                Brs = Buf()
                rstd = k.sb(st, [128, T], F32, "rstd")
                Brstd = Buf()
                tmp = [k.sb(st, [128, T], F32, "tmp") for _ in range(2)]
                Btmp = [Buf(), Buf()]
                gpre = V_F1PRE if f == 0 else V_F2PRE
                bN, bG, bU, bD = 0, (1, 2), (3, 4), (5, 6)
                guidx = 0
                for i in range(NT):
                    p = i % 2
                    Xi, BXi = X[p], BX[p]
                    S.dma("sp", sX[p], Xi[:] if src3d else Xi[:].rearrange("p a b -> p (a b)"), src_fn(i), reads=[Bsrc[i]], writes=BXi)
                    rms_stats(lambda c: Xi[:, c, :], BXi, sq, Bsq, PB[bN], BPB[bN], rs, Brs, rstd, Brstd)
                    for c in range(8):
                        k.stt("dve", U[:, c, :], Xi[:, c, :], vecs[:, gpre, c:c + 1], rstd[:], ALU.mult, ALU.mult,
                              reads=[BXi[c], Brstd, B_vecs], writes=[BU[c]])
                    for j in range(NJ):
                        sl = (guidx // 2) % NSL
                        if j % 2 == 0 and i == 0 and f == 0:
                            guflat = GU[sl][:].rearrange("p a b c d -> p (a b c d)")
                            for jx in (j, j + 1):
                                load_cast(st, lambda lo, hi, o=(jx - j) * 2048: guflat[:, o + lo:o + hi],
                                          wgu_d[f][:, jx * 2048:(jx + 1) * 2048], 2048, [], wbuf=BGU[sl])
                            S.dma("act", sGUst, wgu_s[f][:, j * 2048:(j + 2) * 2048], guflat,
                                  reads=[BGU[sl]], writes=[B_wgu_s[f][j // 2]])
                        elif j % 2 == 0:
                            S.dma("sp", sGU[sl], GU[sl][:].rearrange("p a b c d -> p (a b c d)"),
                                  wgu_s[f][:, j * 2048:(j + 2) * 2048], reads=[B_wgu_s[f][j // 2]], writes=[BGU[sl]])
                        if i == 0:
                            load_cast(st, lambda lo, hi, jw=j: wd[:, jw, lo:hi], wd_d[f][:, j * 1024:(j + 1) * 1024], 1024, B_wd[j])
                        jj = j % 2
                        guidx += 1
                        g_b, u_b = bG[j % 2], bU[j % 2]
                        for kc in range(8):
                            k.mm(PB[g_b][:], GU[sl][:, jj, 0, kc, :], U[:, kc, :], kc == 0, kc == 7,
                                 reads=[BGU[sl], BU[kc]], writes=[BPB[g_b]])
                        for kc in range(8):
                            k.mm(PB[u_b][:], GU[sl][:, jj, 1, kc, :], U[:, kc, :], kc == 0, kc == 7,
                                 reads=[BGU[sl], BU[kc]], writes=[BPB[u_b]])
                        k.act(sg[j % 2][:], PB[g_b][:], AF.Silu, reads=[BPB[g_b]], writes=[Bsg[j % 2]])
                        k.tt("dve", ACTB[:, j, :], sg[j % 2][:], PB[u_b][:], ALU.mult,
                             reads=[Bsg[j % 2], BPB[u_b]], writes=[BACT[j]])
                    for oc in range(8):
                        d_b = bD[oc % 2]
                        for j in range(NJ):
                            k.mm(PB[d_b][:], wd[:, j, oc * 128:(oc + 1) * 128], ACTB[:, j, :], j == 0, j == NJ - 1,
                                 reads=B_wd[j] + [BACT[j]], writes=[BPB[d_b]])
                        k.cp("act", Fb[:, oc, :], PB[d_b][:], reads=[BPB[d_b]], writes=[BF[oc]])
                    rms_stats(lambda c: Fb[:, c, :], BF, sq, Bsq, PB[bN], BPB[bN], rs, Brs, rstd, Brstd)
                    for c in range(8):
                        s = c % 2
                        k.stt("dve", tmp[s][:], Fb[:, c, :], ghalf[:, f, c:c + 1], rstd[:], ALU.mult, ALU.mult,
                              reads=[BF[c], Brstd, B_const], writes=[Btmp[s]])
                        k.tt("pool", Xi[:, c, :], tmp[s][:], Xi[:, c, :], ALU.add, reads=[Btmp[s]], writes=[BXi[c]])
                    S.dma("act", sY[p], dst_fn(i), Xi[:] if dst3d else Xi[:].rearrange("p a b -> p (a b)"),
                          reads=BXi, writes=[Bdst[i]] if isinstance(Bdst, list) else [Bdst])
                    if emit_u2:
                        rms_stats(lambda c: Xi[:, c, :], BXi, sq, Bsq, PB[bN], BPB[bN], rs, Brs, rstd, Brstd)
                        for c in range(8):
                            k.stt("dve", U2[:, c, :], Xi[:, c, :], vecs[:, V_MIXPRE, c:c + 1], rstd[:], ALU.mult, ALU.mult,
                                  reads=[BXi[c], Brstd, B_vecs], writes=[BU2])
                        S.dma("act", sU2, u2s[i], U2[:].rearrange("p a b -> p (a b)"), reads=[BU2], writes=[B_u2s[i]])
                        if i >= 4:
                            S.dma("act", sU2, exb[i - 4], U2[:].rearrange("p a b -> p (a b)"),
                                  reads=[BU2], writes=[B_exb])
                S.barrier()

        def pass_lru():
            with ExitStack() as st:
                wx = k.sb(st, [128, 16, 8, 128], BF16, "wxr")
                B_wx = []
                wxf = wx[:].rearrange("p a b c -> p (a b c)")
                load_cast(st, lambda lo, hi: wxf[:, lo:hi], wfm_d[:, 24 * 1024:32 * 1024], 8 * 1024, B_wx)
                wl = k.sb(st, [128, 2, 4, 2, 256], BF16, "wl")
                B_wl = []
                load_cast(st, lambda lo, hi: wl[:].rearrange("p a b c d -> p (a b c d)")[:, lo:hi], wlx_d, 2048, B_wl)
                load_cast(st, lambda lo, hi: wl[:].rearrange("p a b c d -> p (a b c d)")[:, 2048 + lo:2048 + hi], wla_d, 2048, B_wl)
                load_cast(st, lambda lo, hi: wxf[:, 8 * 1024 + lo:8 * 1024 + hi], wfm_d[:, 32 * 1024:40 * 1024], 8 * 1024, B_wx)
                c8 = k.sb(st, [128, 8], F32, "c8")
                c16 = k.sb(st, [128, 8], F32, "c16")
                B_c8 = Buf()
                k.act(c8[:], vecs[:, V_AP, :], AF.Exp, reads=[B_vecs], writes=[B_c8], scale=-1.0)
                k.act(c8[:], c8[:], AF.Ln, reads=[B_c8, B_const], writes=[B_c8], bias=onec[:], scale=1.0)
                k.ts1("dve", c16[:], c8[:], -16.0, ALU.mult, reads=[B_c8], writes=[B_c8])
                k.ts1("dve", c8[:], c8[:], -8.0, ALU.mult, reads=[B_c8], writes=[B_c8])
                hcar = k.sb(st, [128, 8], F32, "hcar")
                acar = k.sb(st, [128, 8], F32, "acar")
                B_hc = [Buf() for _ in range(8)]
                B_ac = [Buf() for _ in range(8)]
                xh = k.sb(st, [128, 8, 3], F32, "xh")
                B_xh = [Buf() for _ in range(8)]
                zeros = k.sb(st, [128, T], F32, "zeros")
                B_z = Buf()
                k.memset("pool", zeros[:], 0.0, writes=[B_z])
                k.memset("pool", hcar[:], 0.0, writes=B_hc)
                k.memset("pool", acar[:], 1.0, writes=B_ac)
                U2 = [k.sb(st, [128, 8, T], BF16, "U2") for _ in range(2)]
                BU2 = [Buf(), Buf()]
                sU = [k.sem("ldu"), k.sem("ldu")]
                XRS = [k.sb(st, [128, T + 3], F32, "XRS") for _ in range(4)]
                BXRS = [Buf() for _ in range(4)]
                XCs = [k.sb(st, [128, 4, T], F32, "XC") for _ in range(2)]
                XCBs = [k.sb(st, [128, 4, T], BF16, "XCB") for _ in range(2)]
                GXs = [k.sb(st, [128, 4, T], F32, "GX") for _ in range(2)]
                GAs = [k.sb(st, [128, 4, T], F32, "GA") for _ in range(2)]
                AAs = [k.sb(st, [128, 4, T], F32, "AA") for _ in range(2)]
                BXCs = [[Buf() for _ in range(4)] for _ in range(2)]
                BXCBs = [[Buf() for _ in range(4)] for _ in range(2)]
                BGXs = [[Buf() for _ in range(4)] for _ in range(2)]
                BGAs = [[Buf() for _ in range(4)] for _ in range(2)]
                BAAs = [[Buf() for _ in range(4)] for _ in range(2)]
                GY = [k.sb(st, [128, T], F32, "GY") for _ in range(4)]
                BGY = [Buf() for _ in range(4)]
                REC = [k.sb(st, [128, 8, T], BF16, "REC") for _ in range(2)]
                AGB = [k.sb(st, [128, 8, T], BF16, "AGB") for _ in range(2)]
                BREC = [Buf(), Buf()]
                BAG = [Buf(), Buf()]
                sR = [k.sem("strec"), k.sem("strec")]
                sG = [k.sem("stag"), k.sem("stag")]
                bXR, bGX, bGA, bYR = (0, 1), (2, 3), (4, 5), (6, 7)
                S.dma("sp", sU[1], U2[1][:].rearrange("p a b -> p (a b)"), gathu[3][0:128, :],
                      reads=[B_gathu], writes=[BU2[1]])
                for c in range(8):
                    b_ = bXR[c % 2]
                    for kc in range(8):
                        k.mm(PB[b_][:, 0:3], wx[:, c, kc, :], U2[1][:, kc, T - 3:T], kc == 0, kc == 7,
                             reads=[B_wx[c // 2], BU2[1]], writes=[BPB[b_]])
                    k.ts1("dve", xh[:, c, :], PB[b_][:, 0:3], flag[:, 0:1], ALU.mult,
                          reads=[BPB[b_], B_flag], writes=[B_xh[c]])
                units = [(i, hh) for i in range(NT) for hh in range(2)]

                def bufs(u):
                    q_ = u % 2
                    return (XCs[q_], XCBs[q_], GXs[q_], GAs[q_], AAs[q_], BXCs[q_], BXCBs[q_], BGXs[q_], BGAs[q_], BAAs[q_])

                def st_L1(u):
                    i, hh = units[u]
                    p = i % 2
                    Ui = U2[p]
                    if hh == 0:
                        S.dma("sp", sU[p], Ui[:].rearrange("p a b -> p (a b)"), u2s[i], reads=[B_u2s[i]], writes=[BU2[p]])
                    XC, XCB, GX, GA, AA, BXC, BXCB, BGX, BGA, BAA = bufs(u)
                    for cc in range(4):
                        c = 4 * hh + cc
                        b_ = bXR[cc % 2]
                        for kc in range(8):
                            k.mm(PB[b_][:], wx[:, c, kc, :], Ui[:, kc, :], kc == 0, kc == 7,
                                 reads=[B_wx[c // 2], BU2[p]], writes=[BPB[b_]])
                        k.cp("act", XRS[cc][:, 3:T + 3], PB[b_][:], reads=[BPB[b_]], writes=[BXRS[cc]])
                        k.act(XC[:, cc, :], PB[b_][:], AF.Identity, reads=[BPB[b_], B_vecs], writes=[BXC[cc]],
                              bias=vecs[:, V_CB, c:c + 1], scale=vecs[:, V_CW3, c:c + 1])
                    for cc in range(4):
                        c = 4 * hh + cc
                        k.cp("dve", XRS[cc][:, 0:3], xh[:, c, :], reads=[B_xh[c]], writes=[BXRS[cc]])
                    for cc in range(4):
                        c = 4 * hh + cc
                        k.cp("dve", xh[:, c, :], XRS[cc][:, T:T + 3], reads=[BXRS[cc]], writes=[B_xh[c]])
                    for kk in range(3):
                        for cc in range(4):
                            c = 4 * hh + cc
                            k.stt("dve", XC[:, cc, :], XRS[cc][:, kk:T + kk], vecs[:, V_CW0 + kk, c:c + 1], XC[:, cc, :],
                                  ALU.mult, ALU.add, reads=[BXRS[cc], BXC[cc]], writes=[BXC[cc]])
                    for cc in range(4):
                        k.cp("act", XCB[:, cc, :], XC[:, cc, :], reads=[BXC[cc]], writes=[BXCB[cc]])

                def st_L25(u):
                    i, hh = units[u]
                    XC, XCB, GX, GA, AA, BXC, BXCB, BGX, BGA, BAA = bufs(u)
                    for cc in range(4):
                        c = 4 * hh + cc
                        h, jc = c // 2, c % 2
                        l0 = 2 * (cc // 2)
                        gxb, gab = bGX[jc], bGA[jc]
                        for ic in range(2):
                            k.mm(PB[gxb][:], wl[:, 0, h, ic, jc * 128:(jc + 1) * 128], XCB[:, l0 + ic, :], ic == 0, ic == 1,
                                 reads=B_wl + [BXCB[l0 + ic]], writes=[BPB[gxb]])
                        for ic in range(2):
                            k.mm(PB[gab][:], wl[:, 1, h, ic, jc * 128:(jc + 1) * 128], XCB[:, l0 + ic, :], ic == 0, ic == 1,
                                 reads=B_wl + [BXCB[l0 + ic]], writes=[BPB[gab]])
                        k.act(GX[:, cc, :], PB[gxb][:], AF.Sigmoid, reads=[BPB[gxb], B_vecs], writes=[BGX[cc]],
                              bias=vecs[:, V_BX, c:c + 1], scale=1.0)
                        k.act(GA[:, cc, :], PB[gab][:], AF.Sigmoid, reads=[BPB[gab], B_vecs], writes=[BGA[cc]],
                              bias=vecs[:, V_BA, c:c + 1], scale=1.0)
                    for cc in range(4):
                        c = 4 * hh + cc
                        k.act(AA[:, cc, :], GA[:, cc, :], AF.Exp, reads=[BGA[cc], B_c8], writes=[BAA[cc]], scale=c8[:, c:c + 1])
                    for cc in range(4):
                        c = 4 * hh + cc
                        k.act(GA[:, cc, :], GA[:, cc, :], AF.Exp, reads=[BGA[cc], B_c8], writes=[BGA[cc]], scale=c16[:, c:c + 1])
                    for cc in range(4):
                        k.act(GA[:, cc, :], GA[:, cc, :], AF.Sqrt, reads=[BGA[cc], B_const], writes=[BGA[cc]], bias=onec[:], scale=-1.0)

                def st_L6(u):
                    i, hh = units[u]
                    XC, XCB, GX, GA, AA, BXC, BXCB, BGX, BGA, BAA = bufs(u)
                    for cc in range(4):
                        k.tt("dve", GX[:, cc, :], GX[:, cc, :], XC[:, cc, :], ALU.mult, reads=[BGX[cc], BXC[cc]], writes=[BGX[cc]])
                    for cc in range(4):
                        k.tt("dve", GX[:, cc, :], GX[:, cc, :], GA[:, cc, :], ALU.mult, reads=[BGX[cc], BGA[cc]], writes=[BGX[cc]])
                    for cc in range(4):
                        c = 4 * hh + cc
                        k.scan(XC[:, cc, :], AA[:, cc, :], GX[:, cc, :], hcar[:, c:c + 1],
                               reads=[BAA[cc], BGX[cc], B_hc[c]], writes=[BXC[cc]])
                    for cc in range(4):
                        c = 4 * hh + cc
                        k.scan(GA[:, cc, :], AA[:, cc, :], zeros[:], acar[:, c:c + 1],
                               reads=[BAA[cc], B_z, B_ac[c]], writes=[BGA[cc]])
                    for cc in range(4):
                        c = 4 * hh + cc
                        k.cp("dve", hcar[:, c:c + 1], XC[:, cc, T - 1:T], reads=[BXC[cc]], writes=[B_hc[c]])
                    for cc in range(4):
                        c = 4 * hh + cc
                        k.cp("dve", acar[:, c:c + 1], GA[:, cc, T - 1:T], reads=[BGA[cc]], writes=[B_ac[c]])

                def st_L7(u):
                    i, hh = units[u]
                    p = i % 2
                    Ui = U2[p]
                    XC, XCB, GX, GA, AA, BXC, BXCB, BGX, BGA, BAA = bufs(u)
                    for cc in range(4):
                        c = 4 * hh + cc
                        yb = bYR[cc % 2]
                        for kc in range(8):
                            k.mm(PB[yb][:], wx[:, 8 + c, kc, :], Ui[:, kc, :], kc == 0, kc == 7,
                                 reads=[B_wx[(8 + c) // 2], BU2[p]], writes=[BPB[yb]])
                        k.act(GY[cc][:], PB[yb][:], AF.Gelu, reads=[BPB[yb]], writes=[BGY[cc]])
                        k.tt("dve", REC[p][:, c, :], XC[:, cc, :], GY[cc][:], ALU.mult,
                             reads=[BXC[cc], BGY[cc]], writes=[BREC[p]])
                        k.tt("dve", AGB[p][:, c, :], GA[:, cc, :], GY[cc][:], ALU.mult,
                             reads=[BGA[cc], BGY[cc]], writes=[BAG[p]])
                    if hh == 1:
                        S.dma("act", sR[p], recs[i], REC[p][:].rearrange("p a b -> p (a b)"), reads=[BREC[p]], writes=[B_recs[i]])
                        S.dma("act", sG[p], agss[i], AGB[p][:].rearrange("p a b -> p (a b)"), reads=[BAG[p]], writes=[B_agss[i]])

                st_L1(0)
                st_L25(0)
                for u in range(len(units)):
                    if u + 1 < len(units):
                        st_L1(u + 1)
                    st_L6(u)
                    st_L7(u)
                    if u + 1 < len(units):
                        st_L25(u + 1)
                sS = k.sem("ststate")
                S.dma("act", sS, stb, hcar[:], reads=B_hc, writes=[B_stb])
                S.barrier()

        def pass_exchange_state():
            S.barrier()
            sc = k.sem("cc")
            S.coll(sc, [stb_t.ap().opt()], [gaths_t.ap().opt()], PAIRS, reads=[B_stb], writes=[B_gaths])
            S._wait("pool", (sc, 1))
            S.barrier()

        def pass_exchange():
            S.barrier()
            for t_ in range(4):
                sc = k.sem("cc")
                S.coll(sc, [exb_t[t_].ap().opt()], [gathu_t[t_].ap().opt()], PAIRS, reads=[B_exb], writes=[B_gathu])
                S._wait("pool", (sc, 1))
            S.barrier()

        def colsel(g, ap3, lead):
            if g == 0:
                return ap3.rearrange(lead + " (m q) -> " + lead + " m q", m=4)
            if g == 1:
                return ap3.rearrange(lead + " (u m) -> " + lead + " m u", m=4)
            return ap3.rearrange(lead + " (u m l) -> " + lead + " m l u", m=4, l=4)

        def unitview(g, ap):
            if g == 2:
                return ap
            return ap

        def pass_attn(g):
            first, last = (g == 0), (g == 2)
            with ExitStack() as st:
                wq = k.sb(st, [128, 4, 8, 128], BF16, "wq")
                wk = k.sb(st, [128, 4, 8, 128], BF16, "wk")
                wv = k.sb(st, [128, 8, 512], BF16, "wv")
                B_w = []
                load_cast(st, lambda lo, hi: wq[:].rearrange("p a b c -> p (a b c)")[:, lo:hi],
                          wfm_d[:, (4 * g) * 1024:(4 * g + 4) * 1024], 4096, B_w)
                load_cast(st, lambda lo, hi: wk[:].rearrange("p a b c -> p (a b c)")[:, lo:hi],
                          wfm_d[:, (12 + 4 * g) * 1024:(12 + 4 * g + 4) * 1024], 4096, B_w)
                load_cast(st, lambda lo, hi: wv[:].rearrange("p a b -> p (a b)")[:, lo:hi],
                          wv_d[:, g * 4096:(g + 1) * 4096], 4096, B_w)
                nt = NTAB[g]
                EB = k.sb(st, [128, nt, 512], F32, "EB")
                B_EB = Buf()
                sT = k.sem("ldtab")
                with ExitStack() as st2:
                    MT = k.sb(st2, [128, nt, 512], F32, "MT")
                    S.dma("sp", sT, EB[:].rearrange("p a b -> p (a b)"), btab_d[:, TABOFF[g] * 512:(TABOFF[g] + nt) * 512], writes=[B_EB])
                    S.dma("sp", sT, MT[:].rearrange("p a b -> p (a b)"), mtab_d[:, TABOFF[g] * 512:(TABOFF[g] + nt) * 512], writes=[B_EB])
                    for t_ in range(nt):
                        k.act(EB[:, t_, :], EB[:, t_, :], AF.Exp, reads=[B_EB], writes=[B_EB])
                        k.tt("dve", EB[:, t_, :], EB[:, t_, :], MT[:, t_, :], ALU.mult, reads=[B_EB], writes=[B_EB])
                    S.barrier()
                NS = 8 if g == 2 else 2
                KT = k.sb(st, [128, NS, 4, 4, 128], BF16, "KT")
                VV = k.sb(st, [128, NS, 4, 512], BF16, "VV")
                BKV = [Buf() for _ in range(NS)]
                Q = k.sb(st, [128, 4, 4, 128], BF16, "Q")
                BQ = Buf()
                U2 = [k.sb(st, [128, 8, T], BF16, "U2") for _ in range(2)]
                BU2 = [Buf(), Buf()]
                sU = [k.sem("ldu"), k.sem("ldu")]
                UP = k.sb(st, [128, 8, T], BF16, "UP") if g > 0 else None
                BUP = Buf()
                NR = 4
                E = [k.sb(st, [128, 512], F32, "E") for _ in range(NR)]
                BE = [Buf() for _ in range(NR)]
                Pb = [k.sb(st, [128, 512], BF16, "Pb") for _ in range(NR)]
                BP = [Buf() for _ in range(NR)]
                accN = [k.sb(st, [128, 4, T], F32, "accN") for _ in range(2)]
                accD = [k.sb(st, [128, 4, T], F32, "accD") for _ in range(2)]
                Bacc = [Buf(), Buf()]
                sA = [k.sem("ldacc"), k.sem("ldacc")]
                sAs = [k.sem("stacc"), k.sem("stacc")]
                ATT = [k.sb(st, [128, 4, T], BF16, "ATT") for _ in range(2)]
                BATT = [Buf(), Buf()]
                bKV, bS, bNm, bDn = (0, 1), (2, 3, 0, 1), (4, 5), (6, 7)
                ctr = {"kv": 0, "s": 0, "u": 0, "ld": 0}

                def slot(i):
                    return i % NS

                def kv_tile(i, Ui, BUi):
                    sl = slot(i)
                    for h in range(4):
                        b_ = bKV[ctr["kv"] % 2]
                        ctr["kv"] += 1
                        for kc in range(8):
                            k.mm(PB[b_][:], wk[:, h, kc, :], Ui[:, kc, :], kc == 0, kc == 7, reads=B_w + [BUi], writes=[BPB[b_]])
                        dst = KT[:, sl, h, :, :]
                        k.cp("act", dst, PB[b_][:].rearrange("e (m q) -> e m q", m=4), reads=[BPB[b_]], writes=[BKV[sl]])
                    for m in range(4):
                        b_ = bKV[ctr["kv"] % 2]
                        ctr["kv"] += 1
                        for kc in range(8):
                            lhs = Ui[:, kc, m * 128:(m + 1) * 128]
                            k.mm(PB[b_][:], lhs, wv[:, kc, :], kc == 0, kc == 7, reads=B_w + [BUi], writes=[BPB[b_]])
                        k.cp("dve", VV[:, sl, m, :], PB[b_][:], reads=[BPB[b_]], writes=[BKV[sl]])

                def permute(Ui, BUi):
                    if g == 0:
                        return Ui, BUi
                    for kc in range(8):
                        dstv = UP[:, kc, :].rearrange("e (m u) -> e m u", m=4) if g == 1 else \
                            UP[:, kc, :].rearrange("e (m l u) -> e m l u", m=4, l=4)
                        k.cp("act" if kc % 2 else "dve", dstv, colsel(g, Ui[:, kc, :], "e"), reads=[BUi], writes=[BUP])
                    return UP, BUP

                pcs = {"n": 0}
                if g == 0:
                    NPR = 4
                    pstg = [k.sb(st, [128, 2048], F32, "pstg") for _ in range(NPR)]
                    Bpstg = [Buf() for _ in range(NPR)]
                    sPst = [k.sem("ldw2") for _ in range(NPR)]
                    ob = [k.sb(st, [128, 2048], BF16, "ob") for _ in range(NPR)]
                    Bob = [Buf() for _ in range(NPR)]
                    sOb = [k.sem("stw2") for _ in range(NPR)]

                def precast_some(cnt):
                    if g != 0:
                        return
                    for _ in range(cnt):
                        n = pcs["n"]
                        if n >= NJ:
                            return
                        pcs["n"] += 1
                        s_ = n % NPR
                        S.dma("sp", sPst[s_], pstg[s_][:], wgu_d[1][:, n * 2048:(n + 1) * 2048], writes=[Bpstg[s_]])
                        k.cp("pool", ob[s_][:], pstg[s_][:], reads=[Bpstg[s_]], writes=[Bob[s_]])
                        S.dma("act", sOb[s_], wgu_s[1][:, n * 2048:(n + 1) * 2048], ob[s_][:],
                              reads=[Bob[s_]], writes=[B_wgu_s[1][n // 2]])

                halo_tiles = [-1] if g < 2 else [-4, -3, -2, -1]
                for ht in halo_tiles:
                    p = ctr["ld"] % 2
                    ctr["ld"] += 1
                    S.dma("sp", sU[p], U2[p][:].rearrange("p a b -> p (a b)"), gathu[4 + ht][0:128, :],
                          reads=[B_gathu], writes=[BU2[p]])
                    Uh, BUh = permute(U2[p], BU2[p])
                    kv_tile(ht, Uh, BUh)

                for i in range(NT):
                    p = ctr["ld"] % 2
                    ctr["ld"] += 1
                    Ui, BUi = U2[p], BU2[p]
                    S.dma("sp", sU[p], Ui[:].rearrange("p a b -> p (a b)"), u2s[i], reads=[B_u2s[i]], writes=[BUi])
                    pa = i % 2
                    if not first:
                        S.dma("sp", sA[pa], accN[pa][:].rearrange("p a b -> p (a b)"), accNs[i], reads=[B_accs[i]], writes=[Bacc[pa]])
                        S.dma("sp", sA[pa], accD[pa][:].rearrange("p a b -> p (a b)"), accDs[i], reads=[B_accs[i]], writes=[Bacc[pa]])
                    Ui, BUi = permute(Ui, BUi)
                    precast_some(3)
                    kv_tile(i, Ui, BUi)
                    for h in range(4):
                        b_ = bKV[ctr["kv"] % 2]
                        ctr["kv"] += 1
                        for kc in range(8):
                            k.mm(PB[b_][:], wq[:, h, kc, :], Ui[:, kc, :], kc == 0, kc == 7, reads=B_w + [BUi], writes=[BPB[b_]])
                        dst = Q[:, h, :, :]
                        k.cp("act", dst, PB[b_][:].rearrange("e (m q) -> e m q", m=4), reads=[BPB[b_]], writes=[BQ])
                    pairs = []
                    for m in range(4):
                        if g == 0:
                            keys = [(i, m, 0)]
                            keys.append((i, m - 1, 1) if m > 0 else (i - 1, 3, 1))
                        elif g == 1:
                            keys = [(i, m, 0), (i - 1, m, 1)]
                        else:
                            ip = i % 4
                            s0 = i - ip
                            keys = [(s0 + ik, m, ip - ik) for ik in range(ip + 1)]
                            keys += [(s0 - 4 + ik, m, 4 + ip - ik) for ik in range(ip, 4)]
                        nb = bNm[ctr["u"] % 2]
                        db = bDn[ctr["u"] % 2]
                        ctr["u"] += 1
                        for ki, (kt, mk, tab) in enumerate(keys):
                            pairs.append((m, ki, len(keys), kt, mk, tab, nb, db))

                    def emit_S(idx):
                        m, ki, nk, kt, mk, tab, nb, db = pairs[idx]
                        sl = slot(kt)
                        r = ctr["s"] % NR
                        ctr["s"] += 1
                        sb_ = bS[r]
                        for h in range(4):
                            k.mm(PB[sb_][:, h * 128:(h + 1) * 128], KT[:, sl, h, mk, :], Q[:, h, m, :], h == 0, True,
                                 reads=[BKV[sl], BQ], writes=[BPB[sb_]], skip=True)
                        if kt < 0:
                            k.act(E[r][:], PB[sb_][:], AF.Exp, reads=[BPB[sb_], B_flag], writes=[BE[r]],
                                  bias=flag[:, 1:2], scale=float(QSCALE))
                        else:
                            k.act(E[r][:], PB[sb_][:], AF.Exp, reads=[BPB[sb_]], writes=[BE[r]], scale=float(QSCALE))
                        k.tt("dve", Pb[r][:], E[r][:], EB[:, tab, :], ALU.mult, reads=[BE[r], B_EB], writes=[BP[r]])
                        return r

                    def emit_PV(idx, r):
                        m, ki, nk, kt, mk, tab, nb, db = pairs[idx]
                        sl = slot(kt)
                        for h in range(4):
                            k.mm(PB[nb][:, h * 128:(h + 1) * 128], VV[:, sl, mk, h * 128:(h + 1) * 128],
                                 Pb[r][:, h * 128:(h + 1) * 128], ki == 0 and h == 0, ki == nk - 1,
                                 reads=[BKV[sl], BP[r]], writes=[BPB[nb]], skip=True)
                        k.mm(PB[db][:], ones1[:], Pb[r][:], ki == 0, ki == nk - 1,
                             reads=[B_const, BP[r]], writes=[BPB[db]])
                        if ki == nk - 1:
                            nview = PB[nb][:].rearrange("e (h q) -> e h q", h=4)
                            dview = PB[db][:].rearrange("e (h q) -> e h q", h=4)
                            if g == 2:
                                nview = nview.rearrange("e h (l u) -> e h l u", l=4)
                                dview = dview.rearrange("e h (l u) -> e h l u", l=4)
                            oN = colsel(g, accN[pa][:], "e h")[:, :, m]
                            oD = colsel(g, accD[pa][:], "e h")[:, :, m]
                            if first:
                                k.cp("act", oN, nview, reads=[BPB[nb]], writes=[Bacc[pa]])
                                k.cp("dve", oD, dview, reads=[BPB[db]], writes=[Bacc[pa]])
                            else:
                                k.tt("dve", oN, oN, nview, ALU.add, reads=[BPB[nb], Bacc[pa]], writes=[Bacc[pa]])
                                k.tt("dve", oD, oD, dview, ALU.add, reads=[BPB[db], Bacc[pa]], writes=[Bacc[pa]])

                    LA = 2
                    rs_ = {}
                    for idx in range(min(LA, len(pairs))):
                        rs_[idx] = emit_S(idx)
                    for idx in range(len(pairs)):
                        if idx + LA < len(pairs):
                            rs_[idx + LA] = emit_S(idx + LA)
                        emit_PV(idx, rs_[idx])
                    if last:
                        k.recip(accD[pa][:], accD[pa][:], reads=[Bacc[pa]], writes=[Bacc[pa]])
                        k.tt("dve", ATT[pa][:], accN[pa][:], accD[pa][:], ALU.mult, reads=[Bacc[pa]], writes=[BATT[pa]])
                        S.dma("act", sAs[pa], attns[i], ATT[pa][:].rearrange("p a b -> p (a b)"), reads=[BATT[pa]], writes=[B_attns[i]])
                    else:
                        S.dma("act", sAs[pa], accNs[i], accN[pa][:].rearrange("p a b -> p (a b)"), reads=[Bacc[pa]], writes=[B_accs[i]])
                        S.dma("act", sAs[pa], accDs[i], accD[pa][:].rearrange("p a b -> p (a b)"), reads=[Bacc[pa]], writes=[B_accs[i]])
                S.barrier()

        def pass_merge():
            with ExitStack() as st:
                wgl = k.sb(st, [128, 16, 8, 128], BF16, "wgl")
                wab = k.sb(st, [128, 4, 1024], BF16, "wab")
                wrb = k.sb(st, [128, 8, 1024], BF16, "wrb")
                wo = k.sb(st, [128, 8, 1024], BF16, "wo")
                B_wab, B_wrb, B_wo = [], [], []
                B_wgl = [None] * 8
                load_cast(st, lambda lo, hi: wab[:].rearrange("p a b -> p (a b)")[:, lo:hi], wab_d, 4096, B_wab)
                load_cast(st, lambda lo, hi: wrb[:].rearrange("p a b -> p (a b)")[:, lo:hi], wrb_d, 8192, B_wrb)
                wglf = wgl[:].rearrange("p a b c -> p (a b c)")
                for pc in (0, 4, 1, 5, 2, 6, 3, 7):
                    tl = []
                    load_cast(st, lambda lo, hi, o=pc * 2048: wglf[:, o + lo:o + hi],
                              wfm_d[:, 40 * 1024 + pc * 2048:40 * 1024 + (pc + 1) * 2048], 2048, tl)
                    B_wgl[pc] = tl[0]
                load_cast(st, lambda lo, hi: wo[:].rearrange("p a b -> p (a b)")[:, lo:hi], wo_d, 8192, B_wo)
                H = [k.sb(st, [128, 8, T], F32, "H") for _ in range(2)]
                BH = [[Buf() for _ in range(8)] for _ in range(2)]
                U2 = [k.sb(st, [128, 8, T], BF16, "U2") for _ in range(2)]
                AT = [k.sb(st, [128, 4, T], BF16, "AT") for _ in range(2)]
                RC = k.sb(st, [128, 8, T], BF16, "RC")
                AGm = k.sb(st, [128, 8, T], BF16, "AGm")
                BRC = [Buf() for _ in range(8)]
                BAGm = Buf()
                sRC = k.sem("ldrc")
                hinit = k.sb(st, [128, 8], F32, "hinit")
                B_hi = Buf()
                sHi = k.sem("ldhi")
                S.dma("sp", sHi, hinit[:], gaths[0:128, :], reads=[B_gaths], writes=[B_hi])
                k.ts1("dve", hinit[:], hinit[:], flag[:, 0:1], ALU.mult, reads=[B_hi, B_flag], writes=[B_hi])
                BIN = [Buf(), Buf()]
                sIn = [k.sem("ldin"), k.sem("ldin")]
                sO = [k.sem("sth2"), k.sem("sth2")]
                MRG = k.sb(st, [128, 8, T], BF16, "MRG")
                BM = [Buf() for _ in range(8)]
                Yf = k.sb(st, [128, 8, T], F32, "Yf")
                BY = [Buf() for _ in range(8)]
                s0 = [k.sb(st, [128, T], F32, "s0")] * 2
                s1 = [k.sb(st, [128, T], F32, "s1")] * 2
                m1 = [k.sb(st, [128, T], F32, "m1")] * 2
                m2 = [k.sb(st, [128, T], F32, "m2")] * 2
                Bs0, Bs1, Bm1, Bm2 = [Buf()] * 2, [Buf()] * 2, [Buf()] * 2, [Buf()] * 2
                sq = [k.sb(st, [128, T], BF16, "sq") for _ in range(2)]
                Bsq = [Buf(), Buf()]
                rs = k.sb(st, [128, T], F32, "rs")
                Brs = Buf()
                rstd = k.sb(st, [128, T], F32, "rstd")
                Brstd = Buf()
                bG0s, bG1s, bADs, bRDs, bY, bN = (0, 1), (2, 3), (4, 5), (6, 7), (0, 1), 2
                for i in range(NT):
                    p = i % 2
                    S.dma("sp", sIn[p], H[p][:].rearrange("p a b -> p (a b)"), h1s[i], reads=[B_h1s[i]], writes=BH[p])
                    S.dma("sp", sIn[p], U2[p][:].rearrange("p a b -> p (a b)"), u2s[i], reads=[B_u2s[i]], writes=[BIN[p]])
                    S.dma("sp", sIn[p], AT[p][:].rearrange("p a b -> p (a b)"), attns[i], reads=[B_attns[i]], writes=[BIN[p]])
                    S.dma("sp", sRC, RC[:].rearrange("p a b -> p (a b)"), recs[i], reads=[B_recs[i]], writes=BRC)
                    S.dma("sp", sRC, AGm[:].rearrange("p a b -> p (a b)"), agss[i], reads=[B_agss[i]], writes=[BAGm])
                    for c in range(8):
                        k.stt("dve", RC[:, c, :], AGm[:, c, :], hinit[:, c:c + 1], RC[:, c, :], ALU.mult, ALU.add,
                              reads=[BAGm, BRC[c], B_hi], writes=[BRC[c]])
                    for c in range(8):
                        s = c % 2
                        bG0, bG1, bAD, bRD = bG0s[s], bG1s[s], bADs[s], bRDs[s]
                        for kc in range(8):
                            k.mm(PB[bG0][:], wgl[:, c, kc, :], U2[p][:, kc, :], kc == 0, kc == 7, reads=[B_wgl[c // 2], BIN[p]], writes=[BPB[bG0]])
                        for kc in range(8):
                            k.mm(PB[bG1][:], wgl[:, 8 + c, kc, :], U2[p][:, kc, :], kc == 0, kc == 7, reads=[B_wgl[(8 + c) // 2], BIN[p]], writes=[BPB[bG1]])
                        for kc in range(4):
                            k.mm(PB[bAD][:], wab[:, kc, c * 128:(c + 1) * 128], AT[p][:, kc, :], kc == 0, kc == 3, reads=B_wab + [BIN[p]], writes=[BPB[bAD]])
                        for kc in range(8):
                            k.mm(PB[bRD][:], wrb[:, kc, c * 128:(c + 1) * 128], RC[:, kc, :], kc == 0, kc == 7, reads=B_wrb + [BRC[kc]], writes=[BPB[bRD]])
                        k.act(s0[s][:], PB[bG0][:], AF.Sigmoid, reads=[BPB[bG0]], writes=[Bs0[s]])
                        k.act(s1[s][:], PB[bG1][:], AF.Sigmoid, reads=[BPB[bG1]], writes=[Bs1[s]])
                        k.tt("dve", m1[s][:], s0[s][:], PB[bAD][:], ALU.mult, reads=[Bs0[s], BPB[bAD]], writes=[Bm1[s]])
                        k.tt("dve", m2[s][:], s1[s][:], PB[bRD][:], ALU.mult, reads=[Bs1[s], BPB[bRD]], writes=[Bm2[s]])
                        k.tt("pool", MRG[:, c, :], m1[s][:], m2[s][:], ALU.add, reads=[Bm1[s], Bm2[s]], writes=[BM[c]])
                    for oc in range(8):
                        yb = bY[oc % 2]
                        for kc in range(8):
                            k.mm(PB[yb][:], wo[:, kc, oc * 128:(oc + 1) * 128], MRG[:, kc, :], kc == 0, kc == 7, reads=B_wo + [BM[kc]], writes=[BPB[yb]])
                        k.cp("act", Yf[:, oc, :], PB[yb][:], reads=[BPB[yb]], writes=[BY[oc]])
                    rms_stats(lambda c: Yf[:, c, :], BY, sq, Bsq, PB[bN], BPB[bN], rs, Brs, rstd, Brstd)
                    for c in range(8):
                        s = c % 2
                        k.stt("dve", m1[s][:], Yf[:, c, :], vecs[:, V_MIXPOST, c:c + 1], rstd[:], ALU.mult, ALU.mult,
                              reads=[BY[c], Brstd, B_vecs], writes=[Bm1[s]])
                        k.tt("pool", H[p][:, c, :], m1[s][:], H[p][:, c, :], ALU.add, reads=[Bm1[s]], writes=[BH[p][c]])
                    S.dma("act", sO[p], h2s[i], H[p][:].rearrange("p a b -> p (a b)"), reads=BH[p], writes=[B_h2s[i]])
                S.barrier()

        pass_ffn(0, lambda i: xT[:, :, i * T:(i + 1) * T], [Buf() for _ in range(NT)], True, lambda i: h1s[i], B_h1s, False, True)
        pass_exchange()
        pass_lru()
        for g in range(3):
            pass_attn(g)
        pass_exchange_state()
        pass_merge()
        pass_ffn(1, lambda i: h2s[i], B_h2s, False, lambda i: yT[:, :, i * T:(i + 1) * T], B_y, True, False)
        S.barrier()
        print("instructions:", S.ninst, {e: len(S.rec[e]) for e in S.ENGS})
        S.emit()
    return nc


def _t5_bucket(dist):
    import math
    max_exact = 16
    nf = np.maximum(dist, 1).astype(np.float32)
    large = max_exact + (np.log(nf / np.float32(max_exact)) / np.float32(math.log(2048 / max_exact))
                         * np.float32(32 - max_exact)).astype(np.int32)
    large = np.minimum(large, 31)
    return np.where(dist < max_exact, dist, large)


def _tables(rel_bias_table):
    bt = np.zeros((9, 128, 4, 128), np.float32)
    mt = np.zeros((9, 128, 4, 128), np.float32)
    kk = np.arange(128)[:, None]
    qq = np.arange(128)[None, :]
    for g in range(2):
        d = GROUPS[g][1]
        for kb in range(2):
            steps = qq - kk + (128 if kb == 1 else 0)
            valid = (steps >= 0) & (steps <= 128)
            bucket = _t5_bucket(np.maximum(steps, 0) * d)
            for h in range(4):
                bt[2 * g + kb, :, h, :] = np.where(valid, rel_bias_table[bucket, g * 4 + h], 0.0)
                mt[2 * g + kb, :, h, :] = valid
    d = 16
    lk = (np.arange(128) // 32)[:, None]
    uk = (np.arange(128) % 32)[:, None]
    lq = (np.arange(128) // 32)[None, :]
    uq = (np.arange(128) % 32)[None, :]
    for dl in range(5):
        steps = 32 * dl + uq - uk
        valid = (steps >= 0) & (steps <= 128) & (lk == lq)
        bucket = _t5_bucket(np.maximum(steps, 0) * d)
        for h in range(4):
            bt[4 + dl, :, h, :] = np.where(valid, rel_bias_table[bucket, 8 + h], 0.0)
            mt[4 + dl, :, h, :] = valid
    return (np.ascontiguousarray(bt.transpose(1, 0, 2, 3).reshape(128, 9 * 512)),
            np.ascontiguousarray(mt.transpose(1, 0, 2, 3).reshape(128, 9 * 512)))


def _fm(w):
    C = w.shape[1]
    return w.reshape(8, 128, C // 128, 128).transpose(1, 2, 0, 3)


def _prep_weights(inp):
    f32 = np.float32
    out = {}
    for f, nm in ((1, "ffn1"), (2, "ffn2")):
        g = _fm(inp[nm + "_w_gate"][0])
        u = _fm(inp[nm + "_w_up"][0])
        gu = np.stack([g, u], axis=2)
        out["wgu%d" % f] = np.ascontiguousarray(gu.reshape(128, NJ * 2048), dtype=f32)
        wd = inp[nm + "_w_down"][0].reshape(NJ, 128, 1024).transpose(1, 0, 2)
        out["wd%d" % f] = np.ascontiguousarray(wd.reshape(128, NJ * 1024), dtype=f32)
    w_in = inp["w_in"][0]
    q, kk, v = w_in[:, 0:1536], w_in[:, 1536:3072], w_in[:, 3072:4608]
    rest = w_in[:, 4608:]
    fm = np.concatenate([_fm(q), _fm(kk), _fm(rest)], axis=1)
    out["wfm"] = np.ascontiguousarray(fm.reshape(128, 56 * 1024), dtype=f32)
    wv = v.reshape(8, 128, 3, 512).transpose(1, 2, 0, 3)
    out["wv"] = np.ascontiguousarray(wv.reshape(128, 3 * 4096), dtype=f32)
    for nm, key in (("wlx", "lru_w_x"), ("wla", "lru_w_a")):
        w = inp[key][0].reshape(4, 2, 128, 256).transpose(2, 0, 1, 3)
        out[nm] = np.ascontiguousarray(w.reshape(128, 2048), dtype=f32)
    out["wab"] = np.ascontiguousarray(inp["w_attn_branch"][0].reshape(4, 128, 1024).transpose(1, 0, 2).reshape(128, 4096), dtype=f32)
    out["wrb"] = np.ascontiguousarray(inp["w_rec_branch"][0].reshape(8, 128, 1024).transpose(1, 0, 2).reshape(128, 8192), dtype=f32)
    out["wo"] = np.ascontiguousarray(inp["w_out"][0].reshape(8, 128, 1024).transpose(1, 0, 2).reshape(128, 8192), dtype=f32)
    vl = [inp["ffn1_norm_pre"][0], inp["ffn1_norm_post"][0], inp["mix_norm_pre"][0], inp["mix_norm_post"][0],
          inp["ffn2_norm_pre"][0], inp["ffn2_norm_post"][0],
          inp["conv_w"][0][0], inp["conv_w"][0][1], inp["conv_w"][0][2], inp["conv_w"][0][3],
          inp["conv_b"][0], inp["lru_b_x"][0].reshape(-1), inp["lru_b_a"][0].reshape(-1), inp["lru_a_param"][0]]
    vecs = np.stack([np.asarray(v_, f32).reshape(8, 128).T for v_ in vl], axis=1)
    out["vecs"] = np.ascontiguousarray(vecs.reshape(128, NVEC * 8), dtype=f32)
    bt, mt = _tables(np.asarray(inp["rel_bias_table"], f32))
    out["btab"], out["mtab"] = bt, mt
    return out


_CACHE = {}


def kernel(**inputs):
    inp = {k_: np.asarray(v_) for k_, v_ in inputs.items()}
    x = inp["x"].astype(np.float32, copy=False)
    if "nc" not in _CACHE:
        _CACHE["nc"] = build_program()
    nc = _CACHE["nc"]
    wts = _prep_weights(inp)
    in_maps = []
    for c in range(8):
        b, half = c // 2, c % 2
        xs = x[b, half * TOK:(half + 1) * TOK, :]
        xTc = np.ascontiguousarray(xs.reshape(TOK, 8, 128).transpose(2, 1, 0))
        fl = np.zeros((128, 2), np.float32)
        fl[:, 0] = float(half)
        fl[:, 1] = (float(half) - 1.0) * 30000.0
        m = {"xT": xTc, "flag": fl}
        m.update(wts)
        in_maps.append(m)
    res = run_bass_kernel_spmd(nc, in_maps, core_ids=list(range(8)))
    _CACHE["res"] = res
    out = np.empty((4, 2 * TOK, 1024), np.float32)
    for c in range(8):
        b, half = c // 2, c % 2
        yT = np.asarray(res.results[c]["yT"])
        out[b, half * TOK:(half + 1) * TOK, :] = yT.transpose(2, 1, 0).reshape(TOK, 1024)
    return out
```

```python
import numpy as np
from contextlib import ExitStack
import concourse.bass as bass
import concourse.mybir as mybir
from concourse.bass_utils import run_bass_kernel_spmd

F32 = mybir.dt.float32
BF16 = mybir.dt.bfloat16
AF = mybir.ActivationFunctionType
ALU = mybir.AluOpType

T = 512
NT = 8
TOK = 4096
DFF = 2816
NJ = 22
EPS = 1e-6
QSCALE = 1.0 / np.sqrt(128.0)
GROUPS = ((128, 1), (512, 4), (2048, 16))
NTAB = (2, 2, 5)
TABOFF = (0, 2, 4)
V_F1PRE, V_F1POST, V_MIXPRE, V_MIXPOST, V_F2PRE, V_F2POST, V_CW0, V_CW1, V_CW2, V_CW3, V_CB, V_BX, V_BA, V_AP = range(14)
NVEC = 14


class Buf:
    __slots__ = ("name", "w", "r")

    def __init__(self, name=""):
        self.name = name
        self.w = None
        self.r = {}


class Sched:
    ENGS = ("pe", "act", "dve", "pool", "sp")

    def __init__(self, nc, stack):
        self.nc = nc
        self.stack = stack
        self.rec = {e: [] for e in self.ENGS}
        self.sems = {}
        self.cnt = {}
        self.seen = {e: {} for e in self.ENGS}
        for e in self.ENGS:
            self.sems[e] = stack.enter_context(nc.semaphore("s_" + e))
            self.cnt[e] = 0
        self.ninst = 0

    def new_sem(self, name):
        self.sems[name] = self.stack.enter_context(self.nc.semaphore(name))
        self.cnt[name] = 0
        return name

    def _wait(self, e, ev):
        if ev is None:
            return
        k, v = ev
        if k == e and e == "pe":
            return
        if k not in self.ENGS:
            v = max(v, self.cnt[k])
        if self.seen[e].get(k, 0) >= v:
            return
        self.seen[e][k] = v
        sem = self.sems[k]
        self.rec[e].append(lambda eng, sem=sem, v=v: eng.wait_ge(sem, v))
        self.ninst += 1

    def _deps(self, e, reads, writes):
        for b in reads:
            self._wait(e, b.w)
        for b in writes:
            self._wait(e, b.w)
            for k, v in b.r.items():
                self._wait(e, (k, v))

    def _mark(self, ev, reads, writes):
        k, v = ev
        for b in reads:
            b.r[k] = v
        for b in writes:
            b.w = ev
            b.r = {}

    def op(self, e, fn, reads=(), writes=()):
        self._deps(e, reads, writes)
        self.cnt[e] += 1
        sem = self.sems[e]
        self.rec[e].append(lambda eng, fn=fn, sem=sem: fn(eng).then_inc(sem, 1))
        self.ninst += 1
        self._mark((e, self.cnt[e]), reads, writes)

    def dma(self, e, semname, out, in_, reads=(), writes=()):
        self._deps(e, reads, writes)
        self.cnt[semname] += 16
        sem = self.sems[semname]
        self.rec[e].append(lambda eng, out=out, in_=in_, sem=sem: eng.dma_start(out=out, in_=in_).then_inc(sem, 16))
        self.ninst += 1
        self._mark((semname, self.cnt[semname]), reads, writes)

    def coll(self, semname, ins, outs, groups, reads=(), writes=()):
        e = "pool"
        self._deps(e, reads, writes)
        self.cnt[semname] += 1
        sem = self.sems[semname]
        self.rec[e].append(lambda eng: eng.collective_compute(
            "AllGather", ALU.bypass, replica_groups=groups, ins=ins, outs=outs).then_inc(sem, 1))
        self._mark((semname, self.cnt[semname]), reads, writes)

    def barrier(self):
        for e in self.ENGS:
            for k in list(self.cnt.keys()):
                if self.cnt[k] > 0:
                    self._wait(e, (k, self.cnt[k]))

    def emit(self):
        nc = self.nc
        rec = self.rec
        with nc.Block() as block:
            @block.tensor
            def _(eng):
                for f in rec["pe"]:
                    f(eng)

            @block.scalar
            def _(eng):
                for f in rec["act"]:
                    f(eng)

            @block.vector
            def _(eng):
                for f in rec["dve"]:
                    f(eng)

            @block.gpsimd
            def _(eng):
                for f in rec["pool"]:
                    f(eng)

            @block.sync
            def _(eng):
                for f in rec["sp"]:
                    f(eng)


class K:
    def __init__(self, nc, stack):
        self.nc = nc
        self.S = Sched(nc, stack)
        self.stack = stack
        self.nm = 0

    def mm(self, out, lhsT, rhs, start, stop, reads=(), writes=(), skip=False):
        if skip:
            self.S.op("pe", lambda e: e.matmul(out, lhsT=lhsT, rhs=rhs, start=start, stop=stop, skip_group_check=True), reads, writes)
        else:
            self.S.op("pe", lambda e: e.matmul(out, lhsT=lhsT, rhs=rhs, start=start, stop=stop), reads, writes)

    def act(self, out, in_, func, reads=(), writes=(), bias=None, scale=None):
        kw = {}
        if bias is not None:
            kw["bias"] = bias
        if scale is not None:
            kw["scale"] = scale
        self.S.op("act", lambda e: e.activation(out=out, in_=in_, func=func, **kw), reads, writes)

    def tt(self, eng, out, in0, in1, op, reads=(), writes=()):
        self.S.op(eng, lambda e: e.tensor_tensor(out=out, in0=in0, in1=in1, op=op), reads, writes)

    def stt(self, eng, out, in0, scalar, in1, op0, op1, reads=(), writes=()):
        self.S.op(eng, lambda e: e.scalar_tensor_tensor(out=out, in0=in0, scalar=scalar, in1=in1, op0=op0, op1=op1), reads, writes)

    def ts(self, eng, out, in0, s1, s2, op0, op1, reads=(), writes=()):
        self.S.op(eng, lambda e: e.tensor_scalar(out=out, in0=in0, scalar1=s1, scalar2=s2, op0=op0, op1=op1), reads, writes)

    def ts1(self, eng, out, in0, s1, op0, reads=(), writes=()):
        self.S.op(eng, lambda e: e.tensor_single_scalar(out=out, in_=in0, scalar=s1, op=op0), reads, writes)

    def cp(self, eng, out, in_, reads=(), writes=()):
        if eng == "act":
            self.S.op("act", lambda e: e.activation(out=out, in_=in_, func=AF.Copy), reads, writes)
        else:
            self.S.op(eng, lambda e: e.tensor_copy(out=out, in_=in_), reads, writes)

    def memset(self, eng, ap, val, writes=()):
        self.S.op(eng, lambda e: e.memset(ap, val), (), writes)

    def recip(self, out, in_, reads=(), writes=()):
        self.S.op("dve", lambda e: e.reciprocal(out=out, in_=in_), reads, writes)

    def scan(self, out, d0, d1, init, reads=(), writes=()):
        self.S.op("dve", lambda e: e.tensor_tensor_scan(out=out, data0=d0, data1=d1, initial=init, op0=ALU.mult, op1=ALU.add), reads, writes)

    def sb(self, st, shape, dt=F32, name=None):
        self.nm += 1
        return st.enter_context(self.nc.sbuf_tensor("%s_%d" % (name or "t", self.nm), shape, dt))

    def sem(self, name):
        self.nm += 1
        return self.S.new_sem("%s_%d" % (name, self.nm))


def build_program():
    nc = bass.Bass("TRN2", target_bir_lowering=False)
    di = lambda name, shape: nc.dram_tensor(name, shape, F32, kind="ExternalInput").ap()
    xT = di("xT", [128, 8, TOK])
    flagd = di("flag", [128, 2])
    wgu_d = [di("wgu1", [128, NJ * 2048]), di("wgu2", [128, NJ * 2048])]
    wd_d = [di("wd1", [128, NJ * 1024]), di("wd2", [128, NJ * 1024])]
    wfm_d = di("wfm", [128, 56 * 1024])
    wv_d = di("wv", [128, 3 * 4096])
    wlx_d = di("wlx", [128, 2048])
    wla_d = di("wla", [128, 2048])
    wab_d = di("wab", [128, 4096])
    wrb_d = di("wrb", [128, 8192])
    wo_d = di("wo", [128, 8192])
    vecs_d = di("vecs", [128, NVEC * 8])
    btab_d = di("btab", [128, 9 * 512])
    mtab_d = di("mtab", [128, 9 * 512])
    yT = nc.dram_tensor("yT", [128, 8, TOK], F32, kind="ExternalOutput").ap()
    import os
    DEBUG = bool(os.environ.get("KDEBUG"))
    dsc = lambda name, shape, dt: (nc.dram_tensor(name, shape, dt, kind="ExternalOutput").ap() if (DEBUG and not name.startswith("wgu"))
                                   else nc.dram_tensor(name, shape, dt).ap())
    wgu_s = [dsc("wgu1s", [128, NJ * 2048], BF16), dsc("wgu2s", [128, NJ * 2048], BF16)]
    h1s = dsc("h1s", [NT, 128, 8 * T], F32)
    h2s = dsc("h2s", [NT, 128, 8 * T], F32)
    u2s = dsc("u2s", [NT, 128, 8 * T], BF16)
    recs = dsc("recs", [NT, 128, 8 * T], BF16)
    agss = dsc("agss", [NT, 128, 8 * T], BF16)
    attns = dsc("attns", [NT, 128, 4 * T], BF16)
    accNs = dsc("accNs", [NT, 128, 4 * T], F32)
    accDs = dsc("accDs", [NT, 128, 4 * T], F32)
    exb_t = [nc.dram_tensor("exb%d" % t_, [128, 8 * T], BF16) for t_ in range(4)]
    gathu_t = [nc.dram_tensor("gathu%d" % t_, [2 * 128, 8 * T], BF16) for t_ in range(4)]
    stb_t = nc.dram_tensor("stb", [128, 8], F32)
    gaths_t = nc.dram_tensor("gaths", [2 * 128, 8], F32)
    exb = [t_.ap() for t_ in exb_t]
    gathu = [t_.ap() for t_ in gathu_t]
    stb, gaths = stb_t.ap(), gaths_t.ap()
    PAIRS = [[0, 1], [2, 3], [4, 5], [6, 7]]

    with ExitStack() as top:
        k = K(nc, top)
        S = k.S
        B_wgu_s = [[Buf() for _ in range(NJ // 2)] for _ in range(2)]
        B_h1s = [Buf() for _ in range(NT)]
        B_h2s = [Buf() for _ in range(NT)]
        B_u2s = [Buf() for _ in range(NT)]
        B_recs = [Buf() for _ in range(NT)]
        B_agss = [Buf() for _ in range(NT)]
        B_attns = [Buf() for _ in range(NT)]
        B_accs = [Buf() for _ in range(NT)]
        B_exb, B_gathu, B_stb, B_gaths, B_y = Buf(), Buf(), Buf(), Buf(), Buf()

        PB = [top.enter_context(nc.psum_tensor("pb%d" % i, [128, 512], F32)) for i in range(8)]
        BPB = [Buf("pb%d" % i) for i in range(8)]

        vecs = k.sb(top, [128, NVEC, 8], F32, "vecs")
        B_vecs = Buf()
        flag = k.sb(top, [128, 2], F32, "flag")
        B_flag = Buf()
        onesM = k.sb(top, [128, 128], BF16, "onesM")
        ones1 = k.sb(top, [128, 128], BF16, "ones1")
        epsc = k.sb(top, [128, 1], F32, "epsc")
        onec = k.sb(top, [128, 1], F32, "onec")
        ghalf = k.sb(top, [128, 2, 8], F32, "ghalf")
        B_const = Buf()
        s_c = k.sem("ldc")
        S.dma("sp", s_c, vecs[:].rearrange("p a b -> p (a b)"), vecs_d, writes=[B_vecs])
        S.dma("sp", s_c, flag[:], flagd, writes=[B_flag])
        k.memset("pool", onesM[:], 1.0 / 1024.0, writes=[B_const])
        k.memset("pool", ones1[:], 1.0, writes=[B_const])
        k.memset("pool", epsc[:], EPS, writes=[B_const])
        k.memset("pool", onec[:], 1.0, writes=[B_const])
        k.ts1("dve", ghalf[:, 0, :], vecs[:, V_F1POST, :], 0.5, ALU.mult, reads=[B_vecs], writes=[B_const])
        k.ts1("dve", ghalf[:, 1, :], vecs[:, V_F2POST, :], 0.5, ALU.mult, reads=[B_vecs], writes=[B_const])

        cast_rr = [0]

        def cast_engine():
            cast_rr[0] += 1
            return ("dve", "pool", "act")[cast_rr[0] % 3]

        PIECE = 2048
        g_stg = [k.sb(top, [128, PIECE], F32, "stg") for _ in range(2)]
        g_Bs = [Buf(), Buf()]
        g_sm = [k.sem("stg"), k.sem("stg")]
        g_n = [0]

        def load_cast(st, dst_ap_fn, src2d, n_elem, Blist, wbuf=None):
            for lo in range(0, n_elem, PIECE):
                hi = min(n_elem, lo + PIECE)
                s = g_n[0] % 2
                g_n[0] += 1
                S.dma("sp", g_sm[s], g_stg[s][:, 0:hi - lo], src2d[:, lo:hi], writes=[g_Bs[s]])
                b = wbuf if wbuf is not None else Buf()
                k.cp(cast_engine(), dst_ap_fn(lo, hi), g_stg[s][:, 0:hi - lo], reads=[g_Bs[s]], writes=[b])
                Blist.append(b)

        def pass_cast_gu():
            with ExitStack() as st:
                piece = 4096
                stg = [k.sb(st, [128, piece], F32, "stg") for _ in range(2)]
                ob = [k.sb(st, [128, piece], BF16, "ob") for _ in range(2)]
                Bs = [Buf(), Buf()]
                Bo = [Buf(), Buf()]
                sm = [k.sem("p0l"), k.sem("p0l")]
                so = [k.sem("p0s"), k.sem("p0s")]
                n = 0
                for f in range(2):
                    for lo in range(0, NJ * 2048, piece):
                        s = n % 2
                        S.dma("sp", sm[s], stg[s][:], wgu_d[f][:, lo:lo + piece], writes=[Bs[s]])
                        k.cp(cast_engine(), ob[s][:], stg[s][:], reads=[Bs[s]], writes=[Bo[s]])
                        S.dma("act", so[s], wgu_s[f][:, lo:lo + piece], ob[s][:], reads=[Bo[s]], writes=[B_wgu_s[f]])
                        n += 1
                S.barrier()

        def rms_stats(Xap, BX, sq, Bsq, bank, Bbank, rs, Brs, rstd, Brstd):
            for c in range(8):
                s = c % 2
                k.act(sq[s][:], Xap(c), AF.Square, reads=[BX[c]], writes=[Bsq[s]])
                k.mm(bank[:], onesM[:], sq[s][:], c == 0, c == 7, reads=[Bsq[s], B_const], writes=[Bbank])
            k.act(rs[:], bank[:], AF.Sqrt, reads=[Bbank, B_const], writes=[Brs], bias=epsc[:], scale=1.0)
            k.recip(rstd[:], rs[:], reads=[Brs], writes=[Brstd])

        def pass_ffn(f, src_fn, Bsrc, src3d, dst_fn, Bdst, dst3d, emit_u2):
            with ExitStack() as st:
                wd = k.sb(st, [128, NJ, 1024], BF16, "wd")
                B_wd = [[] for _ in range(NJ)]
                sGUst = k.sem("stgu")
                X = [k.sb(st, [128, 8, T], F32, "X") for _ in range(2)]
                BX = [[Buf() for _ in range(8)] for _ in range(2)]
                sX = [k.sem("ldx"), k.sem("ldx")]
                sY = [k.sem("sty"), k.sem("sty")]
                U = k.sb(st, [128, 8, T], BF16, "U")
                BU = [Buf() for _ in range(8)]
                U2 = k.sb(st, [128, 8, T], BF16, "U2")
                BU2 = Buf()
                sU2 = k.sem("stu2")
                ACTB = k.sb(st, [128, NJ, T], BF16, "actb")
                BACT = [Buf() for _ in range(NJ)]
                Fb = k.sb(st, [128, 8, T], F32, "F")
                BF = [Buf() for _ in range(8)]
                NSL = 3
                GU = [k.sb(st, [128, 2, 2, 8, 128], BF16, "gu") for _ in range(NSL)]
                BGU = [Buf() for _ in range(NSL)]
                sGU = [k.sem("ldgu") for _ in range(NSL)]
                sg = [k.sb(st, [128, T], F32, "sg") for _ in range(2)]
                Bsg = [Buf(), Buf()]
                sq = [k.sb(st, [128, T], BF16, "sq") for _ in range(2)]
                Bsq = [Buf(), Buf()]
                rs = k.sb(st, [128, T], F32, "rs")
                Brs = Buf()
                rstd = k.sb(st, [128, T], F32, "rstd")
                Brstd = Buf()
                tmp = [k.sb(st, [128, T], F32, "tmp") for _ in range(2)]
                Btmp = [Buf(), Buf()]
                gpre = V_F1PRE if f == 0 else V_F2PRE
                bN, bG, bU, bD = 0, (1, 2), (3, 4), (5, 6)
                guidx = 0
                for i in range(NT):
                    p = i % 2
                    Xi, BXi = X[p], BX[p]
                    S.dma("sp", sX[p], Xi[:] if src3d else Xi[:].rearrange("p a b -> p (a b)"), src_fn(i), reads=[Bsrc[i]], writes=BXi)
                    rms_stats(lambda c: Xi[:, c, :], BXi, sq, Bsq, PB[bN], BPB[bN], rs, Brs, rstd, Brstd)
                    for c in range(8):
                        k.stt("dve", U[:, c, :], Xi[:, c, :], vecs[:, gpre, c:c + 1], rstd[:], ALU.mult, ALU.mult,
                              reads=[BXi[c], Brstd, B_vecs], writes=[BU[c]])
                    for j in range(NJ):
                        sl = (guidx // 2) % NSL
                        if j % 2 == 0 and i == 0 and f == 0:
                            guflat = GU[sl][:].rearrange("p a b c d -> p (a b c d)")
                            for jx in (j, j + 1):
                                load_cast(st, lambda lo, hi, o=(jx - j) * 2048: guflat[:, o + lo:o + hi],
                                          wgu_d[f][:, jx * 2048:(jx + 1) * 2048], 2048, [], wbuf=BGU[sl])
                            S.dma("act", sGUst, wgu_s[f][:, j * 2048:(j + 2) * 2048], guflat,
                                  reads=[BGU[sl]], writes=[B_wgu_s[f][j // 2]])
                        elif j % 2 == 0:
                            S.dma("sp", sGU[sl], GU[sl][:].rearrange("p a b c d -> p (a b c d)"),
                                  wgu_s[f][:, j * 2048:(j + 2) * 2048], reads=[B_wgu_s[f][j // 2]], writes=[BGU[sl]])
                        if i == 0:
                            load_cast(st, lambda lo, hi, jw=j: wd[:, jw, lo:hi], wd_d[f][:, j * 1024:(j + 1) * 1024], 1024, B_wd[j])
                        jj = j % 2
                        guidx += 1
                        g_b, u_b = bG[j % 2], bU[j % 2]
                        for kc in range(8):
                            k.mm(PB[g_b][:], GU[sl][:, jj, 0, kc, :], U[:, kc, :], kc == 0, kc == 7,
                                 reads=[BGU[sl], BU[kc]], writes=[BPB[g_b]])
                        for kc in range(8):
                            k.mm(PB[u_b][:], GU[sl][:, jj, 1, kc, :], U[:, kc, :], kc == 0, kc == 7,
                                 reads=[BGU[sl], BU[kc]], writes=[BPB[u_b]])
                        k.act(sg[j % 2][:], PB[g_b][:], AF.Silu, reads=[BPB[g_b]], writes=[Bsg[j % 2]])
                        k.tt("dve", ACTB[:, j, :], sg[j % 2][:], PB[u_b][:], ALU.mult,
                             reads=[Bsg[j % 2], BPB[u_b]], writes=[BACT[j]])
                    for oc in range(8):
                        d_b = bD[oc % 2]
                        for j in range(NJ):
                            k.mm(PB[d_b][:], wd[:, j, oc * 128:(oc + 1) * 128], ACTB[:, j, :], j == 0, j == NJ - 1,
                                 reads=B_wd[j] + [BACT[j]], writes=[BPB[d_b]])
                        k.cp("act", Fb[:, oc, :], PB[d_b][:], reads=[BPB[d_b]], writes=[BF[oc]])
                    rms_stats(lambda c: Fb[:, c, :], BF, sq, Bsq, PB[bN], BPB[bN], rs, Brs, rstd, Brstd)
                    for c in range(8):
                        s = c % 2
                        k.stt("dve", tmp[s][:], Fb[:, c, :], ghalf[:, f, c:c + 1], rstd[:], ALU.mult, ALU.mult,
                              reads=[BF[c], Brstd, B_const], writes=[Btmp[s]])
                        k.tt("pool", Xi[:, c, :], tmp[s][:], Xi[:, c, :], ALU.add, reads=[Btmp[s]], writes=[BXi[c]])
                    S.dma("act", sY[p], dst_fn(i), Xi[:] if dst3d else Xi[:].rearrange("p a b -> p (a b)"),
                          reads=BXi, writes=[Bdst[i]] if isinstance(Bdst, list) else [Bdst])
                    if emit_u2:
                        rms_stats(lambda c: Xi[:, c, :], BXi, sq, Bsq, PB[bN], BPB[bN], rs, Brs, rstd, Brstd)
                        for c in range(8):
                            k.stt("dve", U2[:, c, :], Xi[:, c, :], vecs[:, V_MIXPRE, c:c + 1], rstd[:], ALU.mult, ALU.mult,
                                  reads=[BXi[c], Brstd, B_vecs], writes=[BU2])
                        S.dma("act", sU2, u2s[i], U2[:].rearrange("p a b -> p (a b)"), reads=[BU2], writes=[B_u2s[i]])
                        if i >= 4:
                            S.dma("act", sU2, exb[i - 4], U2[:].rearrange("p a b -> p (a b)"),
                                  reads=[BU2], writes=[B_exb])
                S.barrier()

        def pass_lru():
            with ExitStack() as st:
                wx = k.sb(st, [128, 16, 8, 128], BF16, "wxr")
                B_wx = []
                wxf = wx[:].rearrange("p a b c -> p (a b c)")
                load_cast(st, lambda lo, hi: wxf[:, lo:hi], wfm_d[:, 24 * 1024:32 * 1024], 8 * 1024, B_wx)
                wl = k.sb(st, [128, 2, 4, 2, 256], BF16, "wl")
                B_wl = []
                load_cast(st, lambda lo, hi: wl[:].rearrange("p a b c d -> p (a b c d)")[:, lo:hi], wlx_d, 2048, B_wl)
                load_cast(st, lambda lo, hi: wl[:].rearrange("p a b c d -> p (a b c d)")[:, 2048 + lo:2048 + hi], wla_d, 2048, B_wl)
                load_cast(st, lambda lo, hi: wxf[:, 8 * 1024 + lo:8 * 1024 + hi], wfm_d[:, 32 * 1024:40 * 1024], 8 * 1024, B_wx)
                c8 = k.sb(st, [128, 8], F32, "c8")
                c16 = k.sb(st, [128, 8], F32, "c16")
                B_c8 = Buf()
                k.act(c8[:], vecs[:, V_AP, :], AF.Exp, reads=[B_vecs], writes=[B_c8], scale=-1.0)
                k.act(c8[:], c8[:], AF.Ln, reads=[B_c8, B_const], writes=[B_c8], bias=onec[:], scale=1.0)
                k.ts1("dve", c16[:], c8[:], -16.0, ALU.mult, reads=[B_c8], writes=[B_c8])
                k.ts1("dve", c8[:], c8[:], -8.0, ALU.mult, reads=[B_c8], writes=[B_c8])
                hcar = k.sb(st, [128, 8], F32, "hcar")
                acar = k.sb(st, [128, 8], F32, "acar")
                B_hc = [Buf() for _ in range(8)]
                B_ac = [Buf() for _ in range(8)]
                xh = k.sb(st, [128, 8, 3], F32, "xh")
                B_xh = [Buf() for _ in range(8)]
                zeros = k.sb(st, [128, T], F32, "zeros")
                B_z = Buf()
                k.memset("pool", zeros[:], 0.0, writes=[B_z])
                k.memset("pool", hcar[:], 0.0, writes=B_hc)
                k.memset("pool", acar[:], 1.0, writes=B_ac)
                U2 = [k.sb(st, [128, 8, T], BF16, "U2") for _ in range(2)]
                BU2 = [Buf(), Buf()]
                sU = [k.sem("ldu"), k.sem("ldu")]
                XRS = [k.sb(st, [128, T + 3], F32, "XRS") for _ in range(4)]
                BXRS = [Buf() for _ in range(4)]
                XCs = [k.sb(st, [128, 4, T], F32, "XC") for _ in range(2)]
                XCBs = [k.sb(st, [128, 4, T], BF16, "XCB") for _ in range(2)]
                GXs = [k.sb(st, [128, 4, T], F32, "GX") for _ in range(2)]
                GAs = [k.sb(st, [128, 4, T], F32, "GA") for _ in range(2)]
                AAs = [k.sb(st, [128, 4, T], F32, "AA") for _ in range(2)]
                BXCs = [[Buf() for _ in range(4)] for _ in range(2)]
                BXCBs = [[Buf() for _ in range(4)] for _ in range(2)]
                BGXs = [[Buf() for _ in range(4)] for _ in range(2)]
                BGAs = [[Buf() for _ in range(4)] for _ in range(2)]
                BAAs = [[Buf() for _ in range(4)] for _ in range(2)]
                GY = [k.sb(st, [128, T], F32, "GY") for _ in range(4)]
                BGY = [Buf() for _ in range(4)]
                REC = [k.sb(st, [128, 8, T], BF16, "REC") for _ in range(2)]
                AGB = [k.sb(st, [128, 8, T], BF16, "AGB") for _ in range(2)]
                BREC = [Buf(), Buf()]
                BAG = [Buf(), Buf()]
                sR = [k.sem("strec"), k.sem("strec")]
                sG = [k.sem("stag"), k.sem("stag")]
                bXR, bGX, bGA, bYR = (0, 1), (2, 3), (4, 5), (6, 7)
                S.dma("sp", sU[1], U2[1][:].rearrange("p a b -> p (a b)"), gathu[3][0:128, :],
                      reads=[B_gathu], writes=[BU2[1]])
                for c in range(8):
                    b_ = bXR[c % 2]
                    for kc in range(8):
                        k.mm(PB[b_][:, 0:3], wx[:, c, kc, :], U2[1][:, kc, T - 3:T], kc == 0, kc == 7,
                             reads=[B_wx[c // 2], BU2[1]], writes=[BPB[b_]])
                    k.ts1("dve", xh[:, c, :], PB[b_][:, 0:3], flag[:, 0:1], ALU.mult,
                          reads=[BPB[b_], B_flag], writes=[B_xh[c]])
                units = [(i, hh) for i in range(NT) for hh in range(2)]

                def bufs(u):
                    q_ = u % 2
                    return (XCs[q_], XCBs[q_], GXs[q_], GAs[q_], AAs[q_], BXCs[q_], BXCBs[q_], BGXs[q_], BGAs[q_], BAAs[q_])

                def st_L1(u):
                    i, hh = units[u]
                    p = i % 2
                    Ui = U2[p]
                    if hh == 0:
                        S.dma("sp", sU[p], Ui[:].rearrange("p a b -> p (a b)"), u2s[i], reads=[B_u2s[i]], writes=[BU2[p]])
                    XC, XCB, GX, GA, AA, BXC, BXCB, BGX, BGA, BAA = bufs(u)
                    for cc in range(4):
                        c = 4 * hh + cc
                        b_ = bXR[cc % 2]
                        for kc in range(8):
                            k.mm(PB[b_][:], wx[:, c, kc, :], Ui[:, kc, :], kc == 0, kc == 7,
                                 reads=[B_wx[c // 2], BU2[p]], writes=[BPB[b_]])
                        k.cp("act", XRS[cc][:, 3:T + 3], PB[b_][:], reads=[BPB[b_]], writes=[BXRS[cc]])
                        k.act(XC[:, cc, :], PB[b_][:], AF.Identity, reads=[BPB[b_], B_vecs], writes=[BXC[cc]],
                              bias=vecs[:, V_CB, c:c + 1], scale=vecs[:, V_CW3, c:c + 1])
                    for cc in range(4):
                        c = 4 * hh + cc
                        k.cp("dve", XRS[cc][:, 0:3], xh[:, c, :], reads=[B_xh[c]], writes=[BXRS[cc]])
                    for cc in range(4):
                        c = 4 * hh + cc
                        k.cp("dve", xh[:, c, :], XRS[cc][:, T:T + 3], reads=[BXRS[cc]], writes=[B_xh[c]])
                    for kk in range(3):
                        for cc in range(4):
                            c = 4 * hh + cc
                            k.stt("dve", XC[:, cc, :], XRS[cc][:, kk:T + kk], vecs[:, V_CW0 + kk, c:c + 1], XC[:, cc, :],
                                  ALU.mult, ALU.add, reads=[BXRS[cc], BXC[cc]], writes=[BXC[cc]])
                    for cc in range(4):
                        k.cp("act", XCB[:, cc, :], XC[:, cc, :], reads=[BXC[cc]], writes=[BXCB[cc]])

                def st_L25(u):
                    i, hh = units[u]
                    XC, XCB, GX, GA, AA, BXC, BXCB, BGX, BGA, BAA = bufs(u)
                    for cc in range(4):
                        c = 4 * hh + cc
                        h, jc = c // 2, c % 2
                        l0 = 2 * (cc // 2)
                        gxb, gab = bGX[jc], bGA[jc]
                        for ic in range(2):
                            k.mm(PB[gxb][:], wl[:, 0, h, ic, jc * 128:(jc + 1) * 128], XCB[:, l0 + ic, :], ic == 0, ic == 1,
                                 reads=B_wl + [BXCB[l0 + ic]], writes=[BPB[gxb]])
                        for ic in range(2):
                            k.mm(PB[gab][:], wl[:, 1, h, ic, jc * 128:(jc + 1) * 128], XCB[:, l0 + ic, :], ic == 0, ic == 1,
                                 reads=B_wl + [BXCB[l0 + ic]], writes=[BPB[gab]])
                        k.act(GX[:, cc, :], PB[gxb][:], AF.Sigmoid, reads=[BPB[gxb], B_vecs], writes=[BGX[cc]],
                              bias=vecs[:, V_BX, c:c + 1], scale=1.0)
                        k.act(GA[:, cc, :], PB[gab][:], AF.Sigmoid, reads=[BPB[gab], B_vecs], writes=[BGA[cc]],
                              bias=vecs[:, V_BA, c:c + 1], scale=1.0)
                    for cc in range(4):
                        c = 4 * hh + cc
                        k.act(AA[:, cc, :], GA[:, cc, :], AF.Exp, reads=[BGA[cc], B_c8], writes=[BAA[cc]], scale=c8[:, c:c + 1])
                    for cc in range(4):
                        c = 4 * hh + cc
                        k.act(GA[:, cc, :], GA[:, cc, :], AF.Exp, reads=[BGA[cc], B_c8], writes=[BGA[cc]], scale=c16[:, c:c + 1])
                    for cc in range(4):
                        k.act(GA[:, cc, :], GA[:, cc, :], AF.Sqrt, reads=[BGA[cc], B_const], writes=[BGA[cc]], bias=onec[:], scale=-1.0)

                def st_L6(u):
                    i, hh = units[u]
                    XC, XCB, GX, GA, AA, BXC, BXCB, BGX, BGA, BAA = bufs(u)
                    for cc in range(4):
                        k.tt("dve", GX[:, cc, :], GX[:, cc, :], XC[:, cc, :], ALU.mult, reads=[BGX[cc], BXC[cc]], writes=[BGX[cc]])
                    for cc in range(4):
                        k.tt("dve", GX[:, cc, :], GX[:, cc, :], GA[:, cc, :], ALU.mult, reads=[BGX[cc], BGA[cc]], writes=[BGX[cc]])
                    for cc in range(4):
                        c = 4 * hh + cc
                        k.scan(XC[:, cc, :], AA[:, cc, :], GX[:, cc, :], hcar[:, c:c + 1],
                               reads=[BAA[cc], BGX[cc], B_hc[c]], writes=[BXC[cc]])
                    for cc in range(4):
                        c = 4 * hh + cc
                        k.scan(GA[:, cc, :], AA[:, cc, :], zeros[:], acar[:, c:c + 1],
                               reads=[BAA[cc], B_z, B_ac[c]], writes=[BGA[cc]])
                    for cc in range(4):
                        c = 4 * hh + cc
                        k.cp("dve", hcar[:, c:c + 1], XC[:, cc, T - 1:T], reads=[BXC[cc]], writes=[B_hc[c]])
                    for cc in range(4):
                        c = 4 * hh + cc
                        k.cp("dve", acar[:, c:c + 1], GA[:, cc, T - 1:T], reads=[BGA[cc]], writes=[B_ac[c]])

                def st_L7(u):
                    i, hh = units[u]
                    p = i % 2
                    Ui = U2[p]
                    XC, XCB, GX, GA, AA, BXC, BXCB, BGX, BGA, BAA = bufs(u)
                    for cc in range(4):
                        c = 4 * hh + cc
                        yb = bYR[cc % 2]
                        for kc in range(8):
                            k.mm(PB[yb][:], wx[:, 8 + c, kc, :], Ui[:, kc, :], kc == 0, kc == 7,
                                 reads=[B_wx[(8 + c) // 2], BU2[p]], writes=[BPB[yb]])
                        k.act(GY[cc][:], PB[yb][:], AF.Gelu, reads=[BPB[yb]], writes=[BGY[cc]])
                        k.tt("dve", REC[p][:, c, :], XC[:, cc, :], GY[cc][:], ALU.mult,
                             reads=[BXC[cc], BGY[cc]], writes=[BREC[p]])
                        k.tt("dve", AGB[p][:, c, :], GA[:, cc, :], GY[cc][:], ALU.mult,
                             reads=[BGA[cc], BGY[cc]], writes=[BAG[p]])
                    if hh == 1:
                        S.dma("act", sR[p], recs[i], REC[p][:].rearrange("p a b -> p (a b)"), reads=[BREC[p]], writes=[B_recs[i]])
                        S.dma("act", sG[p], agss[i], AGB[p][:].rearrange("p a b -> p (a b)"), reads=[BAG[p]], writes=[B_agss[i]])

                st_L1(0)
                st_L25(0)
                for u in range(len(units)):
                    if u + 1 < len(units):
                        st_L1(u + 1)
                    st_L6(u)
                    st_L7(u)
                    if u + 1 < len(units):
                        st_L25(u + 1)
                sS = k.sem("ststate")
                S.dma("act", sS, stb, hcar[:], reads=B_hc, writes=[B_stb])
                S.barrier()

        def pass_exchange_state():
            S.barrier()
            sc = k.sem("cc")
            S.coll(sc, [stb_t.ap().opt()], [gaths_t.ap().opt()], PAIRS, reads=[B_stb], writes=[B_gaths])
            S._wait("pool", (sc, 1))
            S.barrier()

        def pass_exchange():
            S.barrier()
            for t_ in range(4):
                sc = k.sem("cc")
                S.coll(sc, [exb_t[t_].ap().opt()], [gathu_t[t_].ap().opt()], PAIRS, reads=[B_exb], writes=[B_gathu])
                S._wait("pool", (sc, 1))
            S.barrier()

        def colsel(g, ap3, lead):
            if g == 0:
                return ap3.rearrange(lead + " (m q) -> " + lead + " m q", m=4)
            if g == 1:
                return ap3.rearrange(lead + " (u m) -> " + lead + " m u", m=4)
            return ap3.rearrange(lead + " (u m l) -> " + lead + " m l u", m=4, l=4)

        def unitview(g, ap):
            if g == 2:
                return ap
            return ap

        def pass_attn(g):
            first, last = (g == 0), (g == 2)
            with ExitStack() as st:
                wq = k.sb(st, [128, 4, 8, 128], BF16, "wq")
                wk = k.sb(st, [128, 4, 8, 128], BF16, "wk")
                wv = k.sb(st, [128, 8, 512], BF16, "wv")
                B_w = []
                load_cast(st, lambda lo, hi: wq[:].rearrange("p a b c -> p (a b c)")[:, lo:hi],
                          wfm_d[:, (4 * g) * 1024:(4 * g + 4) * 1024], 4096, B_w)
                load_cast(st, lambda lo, hi: wk[:].rearrange("p a b c -> p (a b c)")[:, lo:hi],
                          wfm_d[:, (12 + 4 * g) * 1024:(12 + 4 * g + 4) * 1024], 4096, B_w)
                load_cast(st, lambda lo, hi: wv[:].rearrange("p a b -> p (a b)")[:, lo:hi],
                          wv_d[:, g * 4096:(g + 1) * 4096], 4096, B_w)
                nt = NTAB[g]
                EB = k.sb(st, [128, nt, 512], F32, "EB")
                B_EB = Buf()
                sT = k.sem("ldtab")
                with ExitStack() as st2:
                    MT = k.sb(st2, [128, nt, 512], F32, "MT")
                    S.dma("sp", sT, EB[:].rearrange("p a b -> p (a b)"), btab_d[:, TABOFF[g] * 512:(TABOFF[g] + nt) * 512], writes=[B_EB])
                    S.dma("sp", sT, MT[:].rearrange("p a b -> p (a b)"), mtab_d[:, TABOFF[g] * 512:(TABOFF[g] + nt) * 512], writes=[B_EB])
                    for t_ in range(nt):
                        k.act(EB[:, t_, :], EB[:, t_, :], AF.Exp, reads=[B_EB], writes=[B_EB])
                        k.tt("dve", EB[:, t_, :], EB[:, t_, :], MT[:, t_, :], ALU.mult, reads=[B_EB], writes=[B_EB])
                    S.barrier()
                NS = 8 if g == 2 else 2
                KT = k.sb(st, [128, NS, 4, 4, 128], BF16, "KT")
                VV = k.sb(st, [128, NS, 4, 512], BF16, "VV")
                BKV = [Buf() for _ in range(NS)]
                Q = k.sb(st, [128, 4, 4, 128], BF16, "Q")
                BQ = Buf()
                U2 = [k.sb(st, [128, 8, T], BF16, "U2") for _ in range(2)]
                BU2 = [Buf(), Buf()]
                sU = [k.sem("ldu"), k.sem("ldu")]
                UP = k.sb(st, [128, 8, T], BF16, "UP") if g > 0 else None
                BUP = Buf()
                NR = 4
                E = [k.sb(st, [128, 512], F32, "E") for _ in range(NR)]
                BE = [Buf() for _ in range(NR)]
                Pb = [k.sb(st, [128, 512], BF16, "Pb") for _ in range(NR)]
                BP = [Buf() for _ in range(NR)]
                accN = [k.sb(st, [128, 4, T], F32, "accN") for _ in range(2)]
                accD = [k.sb(st, [128, 4, T], F32, "accD") for _ in range(2)]
                Bacc = [Buf(), Buf()]
                sA = [k.sem("ldacc"), k.sem("ldacc")]
                sAs = [k.sem("stacc"), k.sem("stacc")]
                ATT = [k.sb(st, [128, 4, T], BF16, "ATT") for _ in range(2)]
                BATT = [Buf(), Buf()]
                bKV, bS, bNm, bDn = (0, 1), (2, 3, 0, 1), (4, 5), (6, 7)
                ctr = {"kv": 0, "s": 0, "u": 0, "ld": 0}

                def slot(i):
                    return i % NS

                def kv_tile(i, Ui, BUi):
                    sl = slot(i)
                    for h in range(4):
                        b_ = bKV[ctr["kv"] % 2]
                        ctr["kv"] += 1
                        for kc in range(8):
                            k.mm(PB[b_][:], wk[:, h, kc, :], Ui[:, kc, :], kc == 0, kc == 7, reads=B_w + [BUi], writes=[BPB[b_]])
                        dst = KT[:, sl, h, :, :]
                        k.cp("act", dst, PB[b_][:].rearrange("e (m q) -> e m q", m=4), reads=[BPB[b_]], writes=[BKV[sl]])
                    for m in range(4):
                        b_ = bKV[ctr["kv"] % 2]
                        ctr["kv"] += 1
                        for kc in range(8):
                            lhs = Ui[:, kc, m * 128:(m + 1) * 128]
                            k.mm(PB[b_][:], lhs, wv[:, kc, :], kc == 0, kc == 7, reads=B_w + [BUi], writes=[BPB[b_]])
                        k.cp("dve", VV[:, sl, m, :], PB[b_][:], reads=[BPB[b_]], writes=[BKV[sl]])

                def permute(Ui, BUi):
                    if g == 0:
                        return Ui, BUi
                    for kc in range(8):
                        dstv = UP[:, kc, :].rearrange("e (m u) -> e m u", m=4) if g == 1 else \
                            UP[:, kc, :].rearrange("e (m l u) -> e m l u", m=4, l=4)
                        k.cp("act" if kc % 2 else "dve", dstv, colsel(g, Ui[:, kc, :], "e"), reads=[BUi], writes=[BUP])
                    return UP, BUP

                pcs = {"n": 0}
                if g == 0:
                    NPR = 4
                    pstg = [k.sb(st, [128, 2048], F32, "pstg") for _ in range(NPR)]
                    Bpstg = [Buf() for _ in range(NPR)]
                    sPst = [k.sem("ldw2") for _ in range(NPR)]
                    ob = [k.sb(st, [128, 2048], BF16, "ob") for _ in range(NPR)]
                    Bob = [Buf() for _ in range(NPR)]
                    sOb = [k.sem("stw2") for _ in range(NPR)]

                def precast_some(cnt):
                    if g != 0:
                        return
                    for _ in range(cnt):
                        n = pcs["n"]
                        if n >= NJ:
                            return
                        pcs["n"] += 1
                        s_ = n % NPR
                        S.dma("sp", sPst[s_], pstg[s_][:], wgu_d[1][:, n * 2048:(n + 1) * 2048], writes=[Bpstg[s_]])
                        k.cp("pool", ob[s_][:], pstg[s_][:], reads=[Bpstg[s_]], writes=[Bob[s_]])
                        S.dma("act", sOb[s_], wgu_s[1][:, n * 2048:(n + 1) * 2048], ob[s_][:],
                              reads=[Bob[s_]], writes=[B_wgu_s[1][n // 2]])

                halo_tiles = [-1] if g < 2 else [-4, -3, -2, -1]
                for ht in halo_tiles:
                    p = ctr["ld"] % 2
                    ctr["ld"] += 1
                    S.dma("sp", sU[p], U2[p][:].rearrange("p a b -> p (a b)"), gathu[4 + ht][0:128, :],
                          reads=[B_gathu], writes=[BU2[p]])
                    Uh, BUh = permute(U2[p], BU2[p])
                    kv_tile(ht, Uh, BUh)

                for i in range(NT):
                    p = ctr["ld"] % 2
                    ctr["ld"] += 1
                    Ui, BUi = U2[p], BU2[p]
                    S.dma("sp", sU[p], Ui[:].rearrange("p a b -> p (a b)"), u2s[i], reads=[B_u2s[i]], writes=[BUi])
                    pa = i % 2
                    if not first:
                        S.dma("sp", sA[pa], accN[pa][:].rearrange("p a b -> p (a b)"), accNs[i], reads=[B_accs[i]], writes=[Bacc[pa]])
                        S.dma("sp", sA[pa], accD[pa][:].rearrange("p a b -> p (a b)"), accDs[i], reads=[B_accs[i]], writes=[Bacc[pa]])
                    Ui, BUi = permute(Ui, BUi)
                    precast_some(3)
                    kv_tile(i, Ui, BUi)
                    for h in range(4):
                        b_ = bKV[ctr["kv"] % 2]
                        ctr["kv"] += 1
                        for kc in range(8):
                            k.mm(PB[b_][:], wq[:, h, kc, :], Ui[:, kc, :], kc == 0, kc == 7, reads=B_w + [BUi], writes=[BPB[b_]])
                        dst = Q[:, h, :, :]
                        k.cp("act", dst, PB[b_][:].rearrange("e (m q) -> e m q", m=4), reads=[BPB[b_]], writes=[BQ])
                    pairs = []
                    for m in range(4):
                        if g == 0:
                            keys = [(i, m, 0)]
                            keys.append((i, m - 1, 1) if m > 0 else (i - 1, 3, 1))
                        elif g == 1:
                            keys = [(i, m, 0), (i - 1, m, 1)]
                        else:
                            ip = i % 4
                            s0 = i - ip
                            keys = [(s0 + ik, m, ip - ik) for ik in range(ip + 1)]
                            keys += [(s0 - 4 + ik, m, 4 + ip - ik) for ik in range(ip, 4)]
                        nb = bNm[ctr["u"] % 2]
                        db = bDn[ctr["u"] % 2]
                        ctr["u"] += 1
                        for ki, (kt, mk, tab) in enumerate(keys):
                            pairs.append((m, ki, len(keys), kt, mk, tab, nb, db))

                    def emit_S(idx):
                        m, ki, nk, kt, mk, tab, nb, db = pairs[idx]
                        sl = slot(kt)
                        r = ctr["s"] % NR
                        ctr["s"] += 1
                        sb_ = bS[r]
                        for h in range(4):
                            k.mm(PB[sb_][:, h * 128:(h + 1) * 128], KT[:, sl, h, mk, :], Q[:, h, m, :], h == 0, True,
                                 reads=[BKV[sl], BQ], writes=[BPB[sb_]], skip=True)
                        if kt < 0:
                            k.act(E[r][:], PB[sb_][:], AF.Exp, reads=[BPB[sb_], B_flag], writes=[BE[r]],
                                  bias=flag[:, 1:2], scale=float(QSCALE))
                        else:
                            k.act(E[r][:], PB[sb_][:], AF.Exp, reads=[BPB[sb_]], writes=[BE[r]], scale=float(QSCALE))
                        k.tt("dve", Pb[r][:], E[r][:], EB[:, tab, :], ALU.mult, reads=[BE[r], B_EB], writes=[BP[r]])
                        return r

                    def emit_PV(idx, r):
                        m, ki, nk, kt, mk, tab, nb, db = pairs[idx]
                        sl = slot(kt)
                        for h in range(4):
                            k.mm(PB[nb][:, h * 128:(h + 1) * 128], VV[:, sl, mk, h * 128:(h + 1) * 128],
                                 Pb[r][:, h * 128:(h + 1) * 128], ki == 0 and h == 0, ki == nk - 1,
                                 reads=[BKV[sl], BP[r]], writes=[BPB[nb]], skip=True)
                        k.mm(PB[db][:], ones1[:], Pb[r][:], ki == 0, ki == nk - 1,
                             reads=[B_const, BP[r]], writes=[BPB[db]])
                        if ki == nk - 1:
                            nview = PB[nb][:].rearrange("e (h q) -> e h q", h=4)
                            dview = PB[db][:].rearrange("e (h q) -> e h q", h=4)
                            if g == 2:
                                nview = nview.rearrange("e h (l u) -> e h l u", l=4)
                                dview = dview.rearrange("e h (l u) -> e h l u", l=4)
                            oN = colsel(g, accN[pa][:], "e h")[:, :, m]
                            oD = colsel(g, accD[pa][:], "e h")[:, :, m]
                            if first:
                                k.cp("act", oN, nview, reads=[BPB[nb]], writes=[Bacc[pa]])
                                k.cp("dve", oD, dview, reads=[BPB[db]], writes=[Bacc[pa]])
                            else:
                                k.tt("dve", oN, oN, nview, ALU.add, reads=[BPB[nb], Bacc[pa]], writes=[Bacc[pa]])
                                k.tt("dve", oD, oD, dview, ALU.add, reads=[BPB[db], Bacc[pa]], writes=[Bacc[pa]])

                    LA = 2
                    rs_ = {}
                    for idx in range(min(LA, len(pairs))):
                        rs_[idx] = emit_S(idx)
                    for idx in range(len(pairs)):
                        if idx + LA < len(pairs):
                            rs_[idx + LA] = emit_S(idx + LA)
                        emit_PV(idx, rs_[idx])
                    if last:
                        k.recip(accD[pa][:], accD[pa][:], reads=[Bacc[pa]], writes=[Bacc[pa]])
                        k.tt("dve", ATT[pa][:], accN[pa][:], accD[pa][:], ALU.mult, reads=[Bacc[pa]], writes=[BATT[pa]])
                        S.dma("act", sAs[pa], attns[i], ATT[pa][:].rearrange("p a b -> p (a b)"), reads=[BATT[pa]], writes=[B_attns[i]])
                    else:
                        S.dma("act", sAs[pa], accNs[i], accN[pa][:].rearrange("p a b -> p (a b)"), reads=[Bacc[pa]], writes=[B_accs[i]])
                        S.dma("act", sAs[pa], accDs[i], accD[pa][:].rearrange("p a b -> p (a b)"), reads=[Bacc[pa]], writes=[B_accs[i]])
                S.barrier()

        def pass_merge():
            with ExitStack() as st:
                wgl = k.sb(st, [128, 16, 8, 128], BF16, "wgl")
                wab = k.sb(st, [128, 4, 1024], BF16, "wab")
                wrb = k.sb(st, [128, 8, 1024], BF16, "wrb")
                wo = k.sb(st, [128, 8, 1024], BF16, "wo")
                B_wab, B_wrb, B_wo = [], [], []
                B_wgl = [None] * 8
                load_cast(st, lambda lo, hi: wab[:].rearrange("p a b -> p (a b)")[:, lo:hi], wab_d, 4096, B_wab)
                load_cast(st, lambda lo, hi: wrb[:].rearrange("p a b -> p (a b)")[:, lo:hi], wrb_d, 8192, B_wrb)
                wglf = wgl[:].rearrange("p a b c -> p (a b c)")
                for pc in (0, 4, 1, 5, 2, 6, 3, 7):
                    tl = []
                    load_cast(st, lambda lo, hi, o=pc * 2048: wglf[:, o + lo:o + hi],
                              wfm_d[:, 40 * 1024 + pc * 2048:40 * 1024 + (pc + 1) * 2048], 2048, tl)
                    B_wgl[pc] = tl[0]
                load_cast(st, lambda lo, hi: wo[:].rearrange("p a b -> p (a b)")[:, lo:hi], wo_d, 8192, B_wo)
                H = [k.sb(st, [128, 8, T], F32, "H") for _ in range(2)]
                BH = [[Buf() for _ in range(8)] for _ in range(2)]
                U2 = [k.sb(st, [128, 8, T], BF16, "U2") for _ in range(2)]
                AT = [k.sb(st, [128, 4, T], BF16, "AT") for _ in range(2)]
                RC = k.sb(st, [128, 8, T], BF16, "RC")
                AGm = k.sb(st, [128, 8, T], BF16, "AGm")
                BRC = [Buf() for _ in range(8)]
                BAGm = Buf()
                sRC = k.sem("ldrc")
                hinit = k.sb(st, [128, 8], F32, "hinit")
                B_hi = Buf()
                sHi = k.sem("ldhi")
                S.dma("sp", sHi, hinit[:], gaths[0:128, :], reads=[B_gaths], writes=[B_hi])
                k.ts1("dve", hinit[:], hinit[:], flag[:, 0:1], ALU.mult, reads=[B_hi, B_flag], writes=[B_hi])
                BIN = [Buf(), Buf()]
                sIn = [k.sem("ldin"), k.sem("ldin")]
                sO = [k.sem("sth2"), k.sem("sth2")]
                MRG = k.sb(st, [128, 8, T], BF16, "MRG")
                BM = [Buf() for _ in range(8)]
                Yf = k.sb(st, [128, 8, T], F32, "Yf")
                BY = [Buf() for _ in range(8)]
                s0 = [k.sb(st, [128, T], F32, "s0")] * 2
                s1 = [k.sb(st, [128, T], F32, "s1")] * 2
                m1 = [k.sb(st, [128, T], F32, "m1")] * 2
                m2 = [k.sb(st, [128, T], F32, "m2")] * 2
                Bs0, Bs1, Bm1, Bm2 = [Buf()] * 2, [Buf()] * 2, [Buf()] * 2, [Buf()] * 2
                sq = [k.sb(st, [128, T], BF16, "sq") for _ in range(2)]
                Bsq = [Buf(), Buf()]
                rs = k.sb(st, [128, T], F32, "rs")
                Brs = Buf()
                rstd = k.sb(st, [128, T], F32, "rstd")
                Brstd = Buf()
                bG0s, bG1s, bADs, bRDs, bY, bN = (0, 1), (2, 3), (4, 5), (6, 7), (0, 1), 2
                for i in range(NT):
                    p = i % 2
                    S.dma("sp", sIn[p], H[p][:].rearrange("p a b -> p (a b)"), h1s[i], reads=[B_h1s[i]], writes=BH[p])
                    S.dma("sp", sIn[p], U2[p][:].rearrange("p a b -> p (a b)"), u2s[i], reads=[B_u2s[i]], writes=[BIN[p]])
                    S.dma("sp", sIn[p], AT[p][:].rearrange("p a b -> p (a b)"), attns[i], reads=[B_attns[i]], writes=[BIN[p]])
                    S.dma("sp", sRC, RC[:].rearrange("p a b -> p (a b)"), recs[i], reads=[B_recs[i]], writes=BRC)
                    S.dma("sp", sRC, AGm[:].rearrange("p a b -> p (a b)"), agss[i], reads=[B_agss[i]], writes=[BAGm])
                    for c in range(8):
                        k.stt("dve", RC[:, c, :], AGm[:, c, :], hinit[:, c:c + 1], RC[:, c, :], ALU.mult, ALU.add,
                              reads=[BAGm, BRC[c], B_hi], writes=[BRC[c]])
                    for c in range(8):
                        s = c % 2
                        bG0, bG1, bAD, bRD = bG0s[s], bG1s[s], bADs[s], bRDs[s]
                        for kc in range(8):
                            k.mm(PB[bG0][:], wgl[:, c, kc, :], U2[p][:, kc, :], kc == 0, kc == 7, reads=[B_wgl[c // 2], BIN[p]], writes=[BPB[bG0]])
                        for kc in range(8):
                            k.mm(PB[bG1][:], wgl[:, 8 + c, kc, :], U2[p][:, kc, :], kc == 0, kc == 7, reads=[B_wgl[(8 + c) // 2], BIN[p]], writes=[BPB[bG1]])
                        for kc in range(4):
                            k.mm(PB[bAD][:], wab[:, kc, c * 128:(c + 1) * 128], AT[p][:, kc, :], kc == 0, kc == 3, reads=B_wab + [BIN[p]], writes=[BPB[bAD]])
                        for kc in range(8):
                            k.mm(PB[bRD][:], wrb[:, kc, c * 128:(c + 1) * 128], RC[:, kc, :], kc == 0, kc == 7, reads=B_wrb + [BRC[kc]], writes=[BPB[bRD]])
                        k.act(s0[s][:], PB[bG0][:], AF.Sigmoid, reads=[BPB[bG0]], writes=[Bs0[s]])
                        k.act(s1[s][:], PB[bG1][:], AF.Sigmoid, reads=[BPB[bG1]], writes=[Bs1[s]])
                        k.tt("dve", m1[s][:], s0[s][:], PB[bAD][:], ALU.mult, reads=[Bs0[s], BPB[bAD]], writes=[Bm1[s]])
                        k.tt("dve", m2[s][:], s1[s][:], PB[bRD][:], ALU.mult, reads=[Bs1[s], BPB[bRD]], writes=[Bm2[s]])
                        k.tt("dve", MRG[:, c, :], m1[s][:], m2[s][:], ALU.add, reads=[Bm1[s], Bm2[s]], writes=[BM[c]])
                    for oc in range(8):
                        yb = bY[oc % 2]
                        for kc in range(8):
                            k.mm(PB[yb][:], wo[:, kc, oc * 128:(oc + 1) * 128], MRG[:, kc, :], kc == 0, kc == 7, reads=B_wo + [BM[kc]], writes=[BPB[yb]])
                        k.cp("act", Yf[:, oc, :], PB[yb][:], reads=[BPB[yb]], writes=[BY[oc]])
                    rms_stats(lambda c: Yf[:, c, :], BY, sq, Bsq, PB[bN], BPB[bN], rs, Brs, rstd, Brstd)
                    for c in range(8):
                        s = c % 2
                        k.stt("dve", m1[s][:], Yf[:, c, :], vecs[:, V_MIXPOST, c:c + 1], rstd[:], ALU.mult, ALU.mult,
                              reads=[BY[c], Brstd, B_vecs], writes=[Bm1[s]])
                        k.tt("pool", H[p][:, c, :], m1[s][:], H[p][:, c, :], ALU.add, reads=[Bm1[s]], writes=[BH[p][c]])
                    S.dma("act", sO[p], h2s[i], H[p][:].rearrange("p a b -> p (a b)"), reads=BH[p], writes=[B_h2s[i]])
                S.barrier()

        pass_ffn(0, lambda i: xT[:, :, i * T:(i + 1) * T], [Buf() for _ in range(NT)], True, lambda i: h1s[i], B_h1s, False, True)
        pass_exchange()
        pass_lru()
        for g in range(3):
            pass_attn(g)
        pass_exchange_state()
        pass_merge()
        pass_ffn(1, lambda i: h2s[i], B_h2s, False, lambda i: yT[:, :, i * T:(i + 1) * T], B_y, True, False)
        S.barrier()
        print("instructions:", S.ninst, {e: len(S.rec[e]) for e in S.ENGS})
        S.emit()
    return nc


def _t5_bucket(dist):
    import math
    max_exact = 16
    nf = np.maximum(dist, 1).astype(np.float32)
    large = max_exact + (np.log(nf / np.float32(max_exact)) / np.float32(math.log(2048 / max_exact))
                         * np.float32(32 - max_exact)).astype(np.int32)
    large = np.minimum(large, 31)
    return np.where(dist < max_exact, dist, large)


def _tables(rel_bias_table):
    bt = np.zeros((9, 128, 4, 128), np.float32)
    mt = np.zeros((9, 128, 4, 128), np.float32)
    kk = np.arange(128)[:, None]
    qq = np.arange(128)[None, :]
    for g in range(2):
        d = GROUPS[g][1]
        for kb in range(2):
            steps = qq - kk + (128 if kb == 1 else 0)
            valid = (steps >= 0) & (steps <= 128)
            bucket = _t5_bucket(np.maximum(steps, 0) * d)
            for h in range(4):
                bt[2 * g + kb, :, h, :] = np.where(valid, rel_bias_table[bucket, g * 4 + h], 0.0)
                mt[2 * g + kb, :, h, :] = valid
    d = 16
    lk = (np.arange(128) // 32)[:, None]
    uk = (np.arange(128) % 32)[:, None]
    lq = (np.arange(128) // 32)[None, :]
    uq = (np.arange(128) % 32)[None, :]
    for dl in range(5):
        steps = 32 * dl + uq - uk
        valid = (steps >= 0) & (steps <= 128) & (lk == lq)
        bucket = _t5_bucket(np.maximum(steps, 0) * d)
        for h in range(4):
            bt[4 + dl, :, h, :] = np.where(valid, rel_bias_table[bucket, 8 + h], 0.0)
            mt[4 + dl, :, h, :] = valid
    return (np.ascontiguousarray(bt.transpose(1, 0, 2, 3).reshape(128, 9 * 512)),
            np.ascontiguousarray(mt.transpose(1, 0, 2, 3).reshape(128, 9 * 512)))


def _fm(w):
    C = w.shape[1]
    return w.reshape(8, 128, C // 128, 128).transpose(1, 2, 0, 3)


def _prep_weights(inp):
    f32 = np.float32
    out = {}
    for f, nm in ((1, "ffn1"), (2, "ffn2")):
        g = _fm(inp[nm + "_w_gate"][0])
        u = _fm(inp[nm + "_w_up"][0])
        gu = np.stack([g, u], axis=2)
        out["wgu%d" % f] = np.ascontiguousarray(gu.reshape(128, NJ * 2048), dtype=f32)
        wd = inp[nm + "_w_down"][0].reshape(NJ, 128, 1024).transpose(1, 0, 2)
        out["wd%d" % f] = np.ascontiguousarray(wd.reshape(128, NJ * 1024), dtype=f32)
    w_in = inp["w_in"][0]
    q, kk, v = w_in[:, 0:1536], w_in[:, 1536:3072], w_in[:, 3072:4608]
    rest = w_in[:, 4608:]
    fm = np.concatenate([_fm(q), _fm(kk), _fm(rest)], axis=1)
    out["wfm"] = np.ascontiguousarray(fm.reshape(128, 56 * 1024), dtype=f32)
    wv = v.reshape(8, 128, 3, 512).transpose(1, 2, 0, 3)
    out["wv"] = np.ascontiguousarray(wv.reshape(128, 3 * 4096), dtype=f32)
    for nm, key in (("wlx", "lru_w_x"), ("wla", "lru_w_a")):
        w = inp[key][0].reshape(4, 2, 128, 256).transpose(2, 0, 1, 3)
        out[nm] = np.ascontiguousarray(w.reshape(128, 2048), dtype=f32)
    out["wab"] = np.ascontiguousarray(inp["w_attn_branch"][0].reshape(4, 128, 1024).transpose(1, 0, 2).reshape(128, 4096), dtype=f32)
    out["wrb"] = np.ascontiguousarray(inp["w_rec_branch"][0].reshape(8, 128, 1024).transpose(1, 0, 2).reshape(128, 8192), dtype=f32)
    out["wo"] = np.ascontiguousarray(inp["w_out"][0].reshape(8, 128, 1024).transpose(1, 0, 2).reshape(128, 8192), dtype=f32)
    vl = [inp["ffn1_norm_pre"][0], inp["ffn1_norm_post"][0], inp["mix_norm_pre"][0], inp["mix_norm_post"][0],
          inp["ffn2_norm_pre"][0], inp["ffn2_norm_post"][0],
          inp["conv_w"][0][0], inp["conv_w"][0][1], inp["conv_w"][0][2], inp["conv_w"][0][3],
          inp["conv_b"][0], inp["lru_b_x"][0].reshape(-1), inp["lru_b_a"][0].reshape(-1), inp["lru_a_param"][0]]
    vecs = np.stack([np.asarray(v_, f32).reshape(8, 128).T for v_ in vl], axis=1)
    out["vecs"] = np.ascontiguousarray(vecs.reshape(128, NVEC * 8), dtype=f32)
    bt, mt = _tables(np.asarray(inp["rel_bias_table"], f32))
    out["btab"], out["mtab"] = bt, mt
    return out


_CACHE = {}


def kernel(**inputs):
    inp = {k_: np.asarray(v_) for k_, v_ in inputs.items()}
    x = inp["x"].astype(np.float32, copy=False)
    if "nc" not in _CACHE:
        _CACHE["nc"] = build_program()
    nc = _CACHE["nc"]
    wts = _prep_weights(inp)
    in_maps = []
    for c in range(8):
        b, half = c // 2, c % 2
        xs = x[b, half * TOK:(half + 1) * TOK, :]
        xTc = np.ascontiguousarray(xs.reshape(TOK, 8, 128).transpose(2, 1, 0))
        fl = np.zeros((128, 2), np.float32)
        fl[:, 0] = float(half)
        fl[:, 1] = (float(half) - 1.0) * 30000.0
        m = {"xT": xTc, "flag": fl}
        m.update(wts)
        in_maps.append(m)
    res = run_bass_kernel_spmd(nc, in_maps, core_ids=list(range(8)))
    _CACHE["res"] = res
    out = np.empty((4, 2 * TOK, 1024), np.float32)
    for c in range(8):
        b, half = c // 2, c % 2
        yT = np.asarray(res.results[c]["yT"])
        out[b, half * TOK:(half + 1) * TOK, :] = yT.transpose(2, 1, 0).reshape(TOK, 1024)
    return out
```
